# Optimizing a Trainium2 kernel written in Bass

```python
import jax, jax.numpy as jnp
from jax import lax
import numpy as np

D_MODEL = 1024
BATCH = 2
SEQ = 8192
DEPTH = 2

HEAD_DIM = 64
ROPE_THETA = 10000.0
NORM_EPS = 1e-6
BLOCK = 128
A_CONFIGS = ((128, 1), (512, 4), (2048, 16))
A_HEADS_PER_GROUP = 8
A_HEADS = A_HEADS_PER_GROUP * len(A_CONFIGS)
A_WIDTH = A_HEADS_PER_GROUP * HEAD_DIM
B_Q_HEADS = 8
B_KV_HEADS = 2
B_GROUP = B_Q_HEADS // B_KV_HEADS
B_WINDOW = 128
B_WIDTH = B_Q_HEADS * HEAD_DIM
C_HEADS = 4
C_DK = 64
C_DV = 128
C_GATE_RANK = 16
C_TAU = 16.0
C_CHUNK = 64
C_WIDTH = C_HEADS * C_DV
N_BRANCH = 3
BRANCH_WIDTH = 512
D_FF = 2816
IN_WIDTHS = (3 * A_HEADS * HEAD_DIM,
             B_Q_HEADS * HEAD_DIM,
             2 * B_KV_HEADS * HEAD_DIM,
             C_HEADS * C_DK,
             C_HEADS * C_DK,
             C_HEADS * C_DV,
             C_GATE_RANK,
             C_WIDTH,
             N_BRANCH * D_MODEL)
IN_WIDTH = sum(IN_WIDTHS)

kernel_name = 'hybrid_gated_dilated_swa_gla'


def rmsnorm(x, g):
    x32 = x.astype(jnp.float32)
    y = x32 * lax.rsqrt(jnp.mean(x32 * x32, axis=-1, keepdims=True) + NORM_EPS)
    return (y * g.astype(jnp.float32)).astype(x.dtype)


def swiglu(h, w_in, w_out):
    g, u = jnp.split(h @ w_in, 2, axis=-1)
    return (jax.nn.silu(g) * u) @ w_out


def rope_tables(positions):
    inv = ROPE_THETA ** (-jnp.arange(0, HEAD_DIM, 2, dtype=jnp.float32) / HEAD_DIM)
    ang = positions.astype(jnp.float32)[..., None] * inv
    return jnp.cos(ang)[:, :, None, :], jnp.sin(ang)[:, :, None, :]


def apply_rope(x, cos, sin):
    x1, x2 = jnp.split(x.astype(jnp.float32), 2, axis=-1)
    return jnp.concatenate([x1 * cos - x2 * sin, x2 * cos + x1 * sin], axis=-1).astype(x.dtype)


def band_attention(q, k, v, window, sink):
    bq, n, hkv, grp, hd = q.shape
    nb = -(-n // BLOCK)
    pad = nb * BLOCK - n
    qb = jnp.pad(q, ((0, 0), (0, pad), (0, 0), (0, 0), (0, 0))).reshape(bq, nb, BLOCK, hkv, grp, hd)

    def key_windows(t):
        tb = jnp.pad(t, ((0, 0), (BLOCK, pad), (0, 0), (0, 0))).reshape(bq, nb + 1, BLOCK, hkv, hd)
        return jnp.concatenate([tb[:, :-1], tb[:, 1:]], axis=2)

    kw, vw = key_windows(k), key_windows(v)
    s = jnp.einsum('bnqkgd,bnskd->bnkgqs', qb, kw).astype(jnp.float32) * (hd ** -0.5)
    qi = jnp.arange(BLOCK)[:, None]
    kj = jnp.arange(2 * BLOCK)[None, :]
    dist = qi + BLOCK - kj
    key_pos = jnp.arange(nb)[:, None, None] * BLOCK + kj - BLOCK
    valid = (dist >= 0) & (dist <= window) & (key_pos >= 0)
    s = jnp.where(valid[None, :, None, None], s, -jnp.inf)
    m = jnp.max(s, axis=-1)
    if sink is not None:
        sk = sink.astype(jnp.float32)[None, None, :, :, None]
        m = jnp.maximum(m, sk)
    p = jnp.exp(s - m[..., None])
    denom = jnp.sum(p, axis=-1)
    if sink is not None:
        denom = denom + jnp.exp(sk - m)
    o = jnp.einsum('bnkgqs,bnskd->bnqkgd', p / denom[..., None], vw.astype(jnp.float32))
    o = o.reshape(bq, nb * BLOCK, hkv, grp, hd)[:, :n].astype(q.dtype)
    lse = (m + jnp.log(denom)).transpose(0, 1, 4, 2, 3).reshape(bq, nb * BLOCK, hkv, grp)[:, :n]
    return o, lse


def dilated_group(q, k, v, window, dilation):
    bn, s, hh, hd = q.shape
    n = s // dilation

    def to_classes(t):
        return t.reshape(bn, n, dilation, hh, hd).transpose(0, 2, 1, 3, 4).reshape(bn * dilation, n, hh, hd)

    o, lse = band_attention(to_classes(q)[:, :, :, None, :], to_classes(k), to_classes(v),
                            window // dilation, None)
    o = o[:, :, :, 0].reshape(bn, dilation, n, hh, hd).transpose(0, 2, 1, 3, 4).reshape(bn, s, hh, hd)
    lse = lse[:, :, :, 0].reshape(bn, dilation, n, hh).transpose(0, 2, 1, 3).reshape(bn, s, hh)
    return o, lse


def dilated_mixer(q, k, v):
    bn, s = q.shape[:2]
    outs, lses = [], []
    for g, (window, dilation) in enumerate(A_CONFIGS):
        sl = slice(g * A_HEADS_PER_GROUP, (g + 1) * A_HEADS_PER_GROUP)
        o, lse = dilated_group(q[:, :, sl], k[:, :, sl], v[:, :, sl], window, dilation)
        outs.append(o)
        lses.append(lse)
    w = jax.nn.softmax(jnp.stack(lses, axis=2), axis=2)
    o = jnp.sum(w[..., None] * jnp.stack(outs, axis=2).astype(jnp.float32), axis=2)
    return o.reshape(bn, s, A_WIDTH).astype(q.dtype)


def gla_chunked(q, k, v, log_g):
    bn, s, hh, dk = q.shape
    dv = v.shape[-1]
    nc = s // C_CHUNK

    def chunks(t):
        return t.astype(jnp.float32).reshape(bn, nc, C_CHUNK, hh, t.shape[-1]).transpose(1, 0, 3, 2, 4)

    qc = chunks(q) * (dk ** -0.5)
    kc, vc, gc = chunks(k), chunks(v), chunks(log_g)
    causal = jnp.tril(jnp.ones((C_CHUNK, C_CHUNK), dtype=bool))[:, :, None]

    def step(state, inp):
        qi, ki, vi, gi = inp
        b = jnp.cumsum(gi, axis=-2)
        o_inter = jnp.einsum('bhtd,bhde->bhte', qi * jnp.exp(b), state)
        diff = b[:, :, :, None, :] - b[:, :, None, :, :]
        decay = jnp.where(causal, jnp.exp(jnp.where(causal, diff, 0.0)), 0.0)
        att = jnp.einsum('bhtd,bhsd,bhtsd->bhts', qi, ki, decay)
        o_intra = jnp.einsum('bhts,bhse->bhte', att, vi)
        b_last = b[:, :, -1:, :]
        new_state = (jnp.exp(b_last[:, :, 0, :])[..., None] * state
                     + jnp.einsum('bhsd,bhse->bhde', ki * jnp.exp(b_last - b), vi))
        return new_state, o_inter + o_intra

    state0 = jnp.zeros((bn, hh, dk, dv), jnp.float32)
    _, o = lax.scan(step, state0, (qc, kc, vc, gc))
    return o.transpose(1, 0, 3, 2, 4).reshape(bn, s, hh, dv).astype(q.dtype)


def hybrid_mixer(h, cos, sin, w_in, a_q_norm, a_k_norm, b_q_norm, b_k_norm, b_sinks,
                 c_gate_up, c_gate_bias, c_out_norm, w_branch, w_out):
    bn, s, _ = h.shape
    splits = np.cumsum(IN_WIDTHS)[:-1].tolist()
    a_qkv, b_q, b_kv, c_q, c_k, c_v, c_glow, c_r, gate_pre = jnp.split(h @ w_in, splits, axis=-1)
    a_qkv = a_qkv.reshape(bn, s, 3, A_HEADS, HEAD_DIM)
    a_q = apply_rope(rmsnorm(a_qkv[:, :, 0], a_q_norm), cos, sin)
    a_k = apply_rope(rmsnorm(a_qkv[:, :, 1], a_k_norm), cos, sin)
    y_a = dilated_mixer(a_q, a_k, a_qkv[:, :, 2])
    b_q = apply_rope(rmsnorm(b_q.reshape(bn, s, B_Q_HEADS, HEAD_DIM), b_q_norm), cos, sin)
    b_kv = b_kv.reshape(bn, s, 2, B_KV_HEADS, HEAD_DIM)
    b_k = apply_rope(rmsnorm(b_kv[:, :, 0], b_k_norm), cos, sin)
    o_b, _ = band_attention(b_q.reshape(bn, s, B_KV_HEADS, B_GROUP, HEAD_DIM), b_k, b_kv[:, :, 1],
                            B_WINDOW - 1, b_sinks.reshape(B_KV_HEADS, B_GROUP))
    y_b = o_b.reshape(bn, s, B_WIDTH)
    log_g = jax.nn.log_sigmoid((c_glow @ c_gate_up + c_gate_bias).astype(jnp.float32)) / C_TAU
    o_c = gla_chunked(c_q.reshape(bn, s, C_HEADS, C_DK), c_k.reshape(bn, s, C_HEADS, C_DK),
                      c_v.reshape(bn, s, C_HEADS, C_DV), log_g.reshape(bn, s, C_HEADS, C_DK))
    y_c = (rmsnorm(o_c, c_out_norm) * jax.nn.silu(c_r.reshape(bn, s, C_HEADS, C_DV))).reshape(bn, s, C_WIDTH)
    gates = jax.nn.sigmoid(gate_pre.reshape(bn, s, N_BRANCH, D_MODEL))
    merged = (gates[:, :, 0] * (y_a @ w_branch[0])
              + gates[:, :, 1] * (y_b @ w_branch[1])
              + gates[:, :, 2] * (y_c @ w_branch[2]))
    return merged @ w_out


def setup_inputs(seed: int = 0) -> dict:
    key = jax.random.key(seed)
    ks = jax.random.split(key, 20)

    def nrm(k, shape, scale):
        return jax.random.normal(k, shape, jnp.float32) * scale

    def gain(k, shape):
        return 1.0 + nrm(k, shape, 0.02)

    L = DEPTH
    return {
        'x': nrm(ks[0], (BATCH, SEQ, D_MODEL), 1.0),
        'positions': jnp.broadcast_to(jnp.arange(SEQ, dtype=jnp.int32), (BATCH, SEQ)),
        'norm_ffn1': gain(ks[1], (L, D_MODEL)),
        'w_ffn1_in': nrm(ks[2], (L, D_MODEL, 2 * D_FF), D_MODEL ** -0.5),
        'w_ffn1_out': nrm(ks[3], (L, D_FF, D_MODEL), D_FF ** -0.5),
        'norm_mix': gain(ks[4], (L, D_MODEL)),
        'w_in': nrm(ks[5], (L, D_MODEL, IN_WIDTH), D_MODEL ** -0.5),
        'a_q_norm': gain(ks[6], (L, HEAD_DIM)),
        'a_k_norm': gain(ks[7], (L, HEAD_DIM)),
        'b_q_norm': gain(ks[8], (L, HEAD_DIM)),
        'b_k_norm': gain(ks[9], (L, HEAD_DIM)),
        'b_sinks': nrm(ks[10], (L, B_Q_HEADS), 1.0),
        'c_gate_up': nrm(ks[11], (L, C_GATE_RANK, C_HEADS * C_DK), C_GATE_RANK ** -0.5),
        'c_gate_bias': nrm(ks[12], (L, C_HEADS * C_DK), 0.1),
        'c_out_norm': gain(ks[13], (L, C_DV)),
        'w_branch': nrm(ks[14], (L, N_BRANCH, BRANCH_WIDTH, D_MODEL), BRANCH_WIDTH ** -0.5),
        'w_out': nrm(ks[15], (L, D_MODEL, D_MODEL), D_MODEL ** -0.5),
        'norm_ffn2': gain(ks[16], (L, D_MODEL)),
        'w_ffn2_in': nrm(ks[17], (L, D_MODEL, 2 * D_FF), D_MODEL ** -0.5),
        'w_ffn2_out': nrm(ks[18], (L, D_FF, D_MODEL), D_FF ** -0.5),
    }


def reference(x, positions, norm_ffn1, w_ffn1_in, w_ffn1_out, norm_mix, w_in, a_q_norm, a_k_norm,
              b_q_norm, b_k_norm, b_sinks, c_gate_up, c_gate_bias, c_out_norm, w_branch, w_out,
              norm_ffn2, w_ffn2_in, w_ffn2_out):
    cos, sin = rope_tables(positions)
    for l in range(DEPTH):
        x = x + 0.5 * swiglu(rmsnorm(x, norm_ffn1[l]), w_ffn1_in[l], w_ffn1_out[l])
        h = rmsnorm(x, norm_mix[l])
        x = x + hybrid_mixer(h, cos, sin, w_in[l], a_q_norm[l], a_k_norm[l], b_q_norm[l], b_k_norm[l],
                             b_sinks[l], c_gate_up[l], c_gate_bias[l], c_out_norm[l], w_branch[l], w_out[l])
        x = x + 0.5 * swiglu(rmsnorm(x, norm_ffn2[l]), w_ffn2_in[l], w_ffn2_out[l])
    return x
```

```python
import numpy as np
import concourse.bass as bass
import concourse.mybir as mybir
from concourse.bass_utils import run_bass_kernel_spmd
from contextlib import ExitStack

F32 = mybir.dt.float32
BF16 = mybir.dt.bfloat16
I32 = mybir.dt.int32
AF = mybir.ActivationFunctionType
ALU = mybir.AluOpType

T = 2048
DM = 1024
DFF = 2816
NFB = DFF // 128
INW = 10000
EPS = 1e-6

SEM_LIMIT = 30000
NDS = 12
SAME_ENGINE_SYNC = True
ATT_QK_SIG = "sync"
MASK_ENG = "dve"
DEN_MERGED = False


class Tk:
    __slots__ = ("w", "r", "name", "wsig")

    def __init__(self, name=""):
        self.w = None
        self.r = {}
        self.name = name
        self.wsig = None


class Sched:
    def __init__(self, nc, es):
        self.nc = nc
        self.es = es
        self.eng = {"pe": nc.tensor, "act": nc.scalar, "dve": nc.vector,
                    "pool": nc.gpsimd, "sp": nc.sync}
        self.sem = {}
        self.cnt = {}
        self.gen = {}
        for k in self.eng:
            self.gen[k] = 0
            self._newsem(k)
        self.seen = {k: {} for k in self.eng}
        self.dsem = [es.enter_context(nc.semaphore(f"dsem{i}")) for i in range(NDS)]
        self.dcnt = [0] * NDS
        self.dnext = 0
        self.nwaits = 0
        self.ninst = 0

    def _newsem(self, k):
        self.sem[k] = self.es.enter_context(self.nc.semaphore(f"sem_{k}_{self.gen[k]}"))
        self.cnt[k] = 0
        self.gen[k] += 1

    def _wait(self, e, deps):
        for (sem, key, n) in deps:
            if key == e and not SAME_ENGINE_SYNC:
                continue
            sk = id(sem)
            if self.seen[e].get(sk, 0) >= n:
                continue
            self.eng[e].wait_ge(sem, n)
            self.nwaits += 1
            self.seen[e][sk] = n

    def _deps(self, reads, writes):
        deps = []
        for t in reads:
            if t.w is not None:
                deps.append(t.w)
        for t in writes:
            if t.w is not None:
                deps.append(t.w)
            deps.extend(t.r.values())
        return deps

    def _mark(self, tk, reads, writes):
        for t in writes:
            t.w = tk
            t.r = {}
        for t in reads:
            t.r[id(tk[0])] = tk

    def op(self, e, fn, reads=(), writes=(), sig=(0, 128)):
        deps = self._deps(reads, writes)
        if e == "pe":
            skip = set()
            for t in writes:
                if t.w is not None and t.w[1] == "pe" and t.wsig == sig and sig != "sync":
                    skip.add(t.w)
            deps = [d for d in deps if not (d[1] == "pe" and d in skip)]
        self._wait(e, deps)
        if self.cnt[e] >= SEM_LIMIT:
            self._newsem(e)
        inst = fn(self.eng[e])
        self.cnt[e] += 1
        inst.then_inc(self.sem[e], 1)
        self.ninst += 1
        tk = (self.sem[e], e, self.cnt[e])
        self._mark(tk, reads, writes)
        if e == "pe":
            for t in writes:
                t.wsig = sig
        return tk

    def cc(self, fn, reads=(), writes=()):
        if not hasattr(self, "ccsem"):
            self.ccsem = self.es.enter_context(self.nc.semaphore("sem_cc"))
            self.cccnt = 0
        self._wait("pool", self._deps(reads, writes))
        inst = fn(self.eng["pool"])
        self.cccnt += 1
        inst.then_inc(self.ccsem, 1)
        self.ninst += 1
        tk = (self.ccsem, "cc", self.cccnt)
        self._mark(tk, reads, writes)
        return tk

    def dma(self, e, out, in_, reads=(), writes=(), **kw):
        i = self.dnext
        self.dnext = (i + 1) % NDS
        deps = self._deps(reads, writes)
        if self.dcnt[i] > 0:
            deps.append((self.dsem[i], "dma", self.dcnt[i]))
        if self.dcnt[i] >= SEM_LIMIT:
            self.dsem[i] = self.es.enter_context(self.nc.semaphore(f"dsem{i}_{self.ninst}"))
            self.dcnt[i] = 0
        self._wait(e, deps)
        inst = self.eng[e].dma_start(out=out, in_=in_, **kw)
        self.dcnt[i] += 16
        inst.then_inc(self.dsem[i], 16)
        self.ninst += 1
        tk = (self.dsem[i], "dma", self.dcnt[i])
        self._mark(tk, reads, writes)
        return tk

    def finish(self, tiles, e="sp"):
        deps = []
        for t in tiles:
            if t.w is not None:
                deps.append(t.w)
            deps.extend(t.r.values())
        self._wait(e, deps)

    def barrier(self):
        deps = [(self.sem[k], k, self.cnt[k]) for k in self.eng if self.cnt[k] > 0]
        deps += [(self.dsem[i], "dma", self.dcnt[i]) for i in range(NDS) if self.dcnt[i] > 0]
        for e in self.eng:
            self._wait(e, [d for d in deps if d[1] != e])


class KC:
    def __init__(self, nc, es):
        self.nc = nc
        self.es = es
        self.S = Sched(nc, es)
        self.ps = [es.enter_context(nc.psum_tensor(f"psb{i}", [128, 512], F32)) for i in range(8)]
        self.pst = [Tk(f"ps{i}") for i in range(8)]
        self.n_uid = 0
        self.dram_tk = {}

    def uid(self, s):
        self.n_uid += 1
        return f"{s}_{self.n_uid}"

    def sb(self, es, name, shape, dt):
        return es.enter_context(self.nc.sbuf_tensor(self.uid(name), shape, dt))

    def dtk(self, name):
        if name not in self.dram_tk:
            self.dram_tk[name] = Tk(name)
        return self.dram_tk[name]


def load_consts(K, es, consts_d):
    nc, S = K.nc, K.S
    c = {}
    c["cb"] = K.sb(es, "cb", [128, consts_d.shape[1]], BF16)
    c["cb_tk"] = Tk("cb")
    S.dma("pool", c["cb"][:], consts_d[:, :], writes=[c["cb_tk"]])
    c["cf"] = K.sb(es, "cf", [128, 8], F32)
    c["cf_tk"] = Tk("cf")
    S.op("dve", lambda e: e.memset(c["cf"][:, 0:1], EPS), writes=[c["cf_tk"]])
    S.op("dve", lambda e: e.memset(c["cf"][:, 1:2], 1.0), writes=[c["cf_tk"]])
    S.op("dve", lambda e: e.memset(c["cf"][:, 2:3], 0.0), writes=[c["cf_tk"]])
    c["ones"] = K.sb(es, "ones", [128, 128], BF16)
    c["ones_tk"] = Tk("ones")
    S.op("dve", lambda e: e.memset(c["ones"][:], 1.0), writes=[c["ones_tk"]])
    return c


C_IDENT = 0
C_PERM = 128
C_BONES = 256
C_MASKA = 384
C_MASKB = 640
C_MASKX = 896
C_TRIL = 1152
NCONST = 1216


def host_consts():
    c = np.zeros((128, NCONST), np.float32)
    c[:, C_IDENT:C_IDENT + 128] = np.eye(128)
    for p in range(128):
        m = p + 32 if (p % 64) < 32 else p - 32
        c[p, C_PERM + m] = 1.0
        c[p, C_BONES + (p // 64) * 64: C_BONES + (p // 64) * 64 + 64] = 1.0
    k = np.arange(128)[:, None]
    q = np.arange(128)[None, :]
    c[:, C_MASKA:C_MASKA + 128] = np.where(k >= q, 1.0, 0.0)
    c[:, C_MASKA + 128:C_MASKA + 256] = np.where(k <= q, 1.0, 0.0)
    c[:, C_MASKB:C_MASKB + 128] = np.where(k > q, 1.0, 0.0)
    c[:, C_MASKB + 128:C_MASKB + 256] = np.where(k <= q, 1.0, 0.0)
    c[:, C_MASKX:C_MASKX + 128] = 0.0
    c[:, C_MASKX + 128:C_MASKX + 256] = np.where(k <= q, 1.0, 0.0)
    c[:64, C_TRIL:C_TRIL + 64] = np.where(k[:64] <= q[:, :64], 1.0, 0.0)
    return c


def rmsnorm_fm(K, C, W, xs, xs_tk, c0, gcol, gcol_tk, out_fn, out_tk, psb):
    S = K.S
    sq, sq_tk, lnv, lnv_tk, rstd, rstd_tk = W["sq"], W["sq_tk"], W["lnv"], W["lnv_tk"], W["rstd"], W["rstd_tk"]
    for kc in range(8):
        S.op("dve", lambda e, kc=kc: e.tensor_tensor(sq[:, kc, :], xs[:, kc, c0:c0 + 512], xs[:, kc, c0:c0 + 512], ALU.mult),
             reads=[xs_tk], writes=[sq_tk])
    for kc in range(8):
        S.op("pe", lambda e, kc=kc: e.matmul(K.ps[psb][:], C["ones"][:], sq[:, kc, :], start=(kc == 0), stop=(kc == 7)),
             reads=[sq_tk, C["ones_tk"]], writes=[K.pst[psb]])
    S.op("act", lambda e: e.activation(lnv[:], K.ps[psb][:], AF.Ln, bias=C["cf"][:, 0:1], scale=1.0 / DM),
         reads=[K.pst[psb], C["cf_tk"]], writes=[lnv_tk])
    S.op("act", lambda e: e.activation(rstd[:], lnv[:], AF.Exp, scale=-0.5), reads=[lnv_tk], writes=[rstd_tk])
    for kc in range(8):
        S.op("dve", lambda e, kc=kc: e.scalar_tensor_tensor(out_fn(kc), xs[:, kc, c0:c0 + 512], gcol[:, kc:kc + 1], rstd[:],
                                                           ALU.mult, ALU.mult),
             reads=[xs_tk, gcol_tk, rstd_tk], writes=[out_tk])


def load_gvec(K, es, name, gd, l):
    g = K.sb(es, name, [128, 8], F32)
    tk = Tk(name)
    src = bass.AP(gd.tensor if hasattr(gd, "tensor") else gd, l * DM, [[1, 128], [128, 8]])
    K.S.dma("sp", g[:], src, writes=[tk], allow_slow_non_contiguous=True)
    return g, tk


def ffn_phase(K, C, l, x_in, x_out, x_tk_in, x_tk_out, g_d, w1_d, w2_d, pre=None, post=None):
    nc, S = K.nc, K.S
    ST = 1024
    with ExitStack() as es:
        xs = K.sb(es, "xs", [128, 8, ST], F32)
        xs_tk = Tk("xs")
        hT = K.sb(es, "hT", [128, 8, ST], BF16)
        hT_tk = [Tk("hT0"), Tk("hT1")]
        aT = K.sb(es, "aT", [128, NFB, ST], BF16)
        aT_tk = [[Tk(f"aT{i}_{t}") for t in range(2)] for i in range(NFB)]
        W = {}
        W["sq"] = K.sb(es, "sq", [128, 8, 512], BF16); W["sq_tk"] = Tk("sq")
        W["lnv"] = K.sb(es, "lnv", [128, 512], F32); W["lnv_tk"] = Tk("lnv")
        W["rstd"] = K.sb(es, "rstd", [128, 512], F32); W["rstd_tk"] = Tk("rstd")
        sg = [K.sb(es, f"sg{i}", [128, 512], F32) for i in range(3)]
        sg_tk = [Tk(f"sg{i}") for i in range(3)]
        w1g = [K.sb(es, f"w1g{i}", [128, 8, 512], BF16) for i in range(2)]
        w1u = [K.sb(es, f"w1u{i}", [128, 8, 512], BF16) for i in range(2)]
        w1_tk = [Tk("w1_0"), Tk("w1_1")]
        w2b = [K.sb(es, f"w2b{i}", [128, NFB, 128], BF16) for i in range(2)]
        w2_tk = [Tk("w2_0"), Tk("w2_1")]
        gcol, gcol_tk = load_gvec(K, es, "gffn", g_d, l)
        if post is not None:
            g2col, g2_tk = load_gvec(K, es, "gpost", post["g"], l)
        if pre is not None:
            mTs = K.sb(es, "mTs", [128, 8, ST], BF16); mTs_tk = Tk()
            wo = [K.sb(es, f"wo{i}", [128, 8, 128], BF16) for i in range(2)]
            wo_tk = [Tk("wo0"), Tk("wo1")]
        w1v = w1_d
        nw1 = 0
        nw2 = 0
        nwo = 0
        pair = 0
        for st in range(T // ST):
            t0 = st * ST
            S.dma("sp", xs[:], x_in[:, t0:t0 + ST].rearrange("(kc p) t -> p kc t", p=128), reads=[x_tk_in], writes=[xs_tk])
            if pre is not None:
                mT, mT_tk = mTs, mTs_tk
                S.dma("sp", mTs[:], pre["mT_d"][:, t0:t0 + ST].rearrange("(kc p) t -> p kc t", p=128), reads=[pre["mT_d_tk"]], writes=[mTs_tk])
                for db in range(8):
                    b = nwo % 2
                    nwo += 1
                    S.dma("pool", wo[b][:], pre["w_out"][l, :, db * 128:(db + 1) * 128].rearrange("(kc p) c -> p kc c", p=128),
                          writes=[wo_tk[b]])
                    for tt in range(2):
                        pb = 6 + tt
                        for kc in range(8):
                            S.op("pe", lambda e, kc=kc, b=b, pb=pb, tt=tt: e.matmul(
                                K.ps[pb][:], wo[b][:, kc, :], mT[:, kc, tt * 512:tt * 512 + 512],
                                start=(kc == 0), stop=(kc == 7)), reads=[wo_tk[b], mT_tk], writes=[K.pst[pb]])
                        S.op("dve", lambda e, db=db, pb=pb, tt=tt: e.tensor_tensor(
                            xs[:, db, tt * 512:tt * 512 + 512], xs[:, db, tt * 512:tt * 512 + 512], K.ps[pb][:], ALU.add),
                            reads=[K.pst[pb], xs_tk], writes=[xs_tk])
            for tt in range(2):
                rmsnorm_fm(K, C, W, xs, xs_tk, tt * 512, gcol, gcol_tk,
                           lambda kc, tt=tt: hT[:, kc, tt * 512:tt * 512 + 512], hT_tk[tt], 6)
            for s0 in range(0, NFB, 4):
                nb = min(4, NFB - s0)
                b = nw1 % 2
                nw1 += 1
                S.dma("pool", w1g[b][:, :, 0:nb * 128],
                      w1v[l, :, s0 * 128:(s0 + nb) * 128].rearrange("(kc p) c -> p kc c", p=128), writes=[w1_tk[b]])
                S.dma("pool", w1u[b][:, :, 0:nb * 128],
                      w1v[l, :, DFF + s0 * 128:DFF + (s0 + nb) * 128].rearrange("(kc p) c -> p kc c", p=128), writes=[w1_tk[b]])
                for i in range(nb):
                    fb = s0 + i
                    for tt in range(2):
                        pg, pu = 2 * (pair % 3), 2 * (pair % 3) + 1
                        sgi = pair % 3
                        pair += 1
                        for (pb, wt) in ((pg, w1g), (pu, w1u)):
                            for kc in range(8):
                                S.op("pe", lambda e, kc=kc, pb=pb, wt=wt, b=b, i=i, tt=tt: e.matmul(
                                    K.ps[pb][:], wt[b][:, kc, i * 128:(i + 1) * 128], hT[:, kc, tt * 512:tt * 512 + 512],
                                    start=(kc == 0), stop=(kc == 7)), reads=[w1_tk[b], hT_tk[tt]], writes=[K.pst[pb]])
                        S.op("act", lambda e, pg=pg, sgi=sgi: e.activation(sg[sgi][:], K.ps[pg][:], AF.Silu),
                             reads=[K.pst[pg]], writes=[sg_tk[sgi]])
                        S.op("dve", lambda e, pu=pu, sgi=sgi, fb=fb, tt=tt: e.tensor_tensor(
                            aT[:, fb, tt * 512:tt * 512 + 512], sg[sgi][:], K.ps[pu][:], ALU.mult),
                            reads=[sg_tk[sgi], K.pst[pu]], writes=[aT_tk[fb][tt]])
            for db in range(8):
                b = nw2 % 2
                nw2 += 1
                S.dma("pool", w2b[b][:], w2_d[l, :, db * 128:(db + 1) * 128].rearrange("(f p) c -> p f c", p=128),
                      writes=[w2_tk[b]])
                for tt in range(2):
                    pb = 6 + tt
                    for f in range(NFB):
                        S.op("pe", lambda e, f=f, b=b, pb=pb, tt=tt: e.matmul(
                            K.ps[pb][:], w2b[b][:, f, :], aT[:, f, tt * 512:tt * 512 + 512],
                            start=(f == 0), stop=(f == NFB - 1)), reads=[w2_tk[b], aT_tk[f][tt]], writes=[K.pst[pb]])
                    S.op("dve", lambda e, db=db, pb=pb, tt=tt: e.scalar_tensor_tensor(
                        xs[:, db, tt * 512:tt * 512 + 512], K.ps[pb][:], 0.5, xs[:, db, tt * 512:tt * 512 + 512],
                        ALU.mult, ALU.add), reads=[K.pst[pb], xs_tk], writes=[xs_tk])
            S.dma("sp", x_out[:, t0:t0 + ST].rearrange("(kc p) t -> p kc t", p=128), xs[:], reads=[xs_tk], writes=[x_tk_out])
            if post is not None:
                hm, hm_tk = post["hT"], post["hT_tk"]
                for tt in range(2):
                    rmsnorm_fm(K, C, W, xs, xs_tk, tt * 512, g2col, g2_tk,
                               lambda kc, tt=tt: hm[:, kc, t0 + tt * 512:t0 + tt * 512 + 512], hm_tk, 6)
                if post.get("h_out") is not None:
                    S.dma("sp", post["h_out"][:, t0:t0 + ST].rearrange("(kc p) t -> p kc t", p=128), hm[:, :, t0:t0 + ST],
                          reads=[hm_tk], writes=[post["h_out_tk"]])
        S.barrier()


A_DIL = (1, 4, 16)
KH_OFF = (0, 128, 640)
KH_W = 2688
VH_OFF = (0, 1, 5)
NH_KA = 0
NH_VA = 4 * KH_W
NH_KB = NH_VA + 21 * 512
NH_VB = NH_KB + 128
NH = NH_VB + 128
NG = 4 * 129
PW = 2176
NPIECE = NH // PW


def exp_segments(c0, w):
    out = []
    o = 0
    while w > 0:
        k = c0 // PW
        lw = min(w, (k + 1) * PW - c0)
        out.append((k, c0 - k * PW, lw, o))
        c0 += lw
        o += lw
        w -= lw
    return out
CO_AQ, CO_AK, CO_AV = 0, 1536, 3072
CO_BQ, CO_BK, CO_BV = 4608, 5120, 5248
CO_CQ, CO_CK, CO_CV, CO_GL, CO_CR, CO_GATE = 5376, 5632, 5888, 6400, 6416, 6928
MAGIC = 12582912.0
TWO_PI = 6.283185307179586


def rope_tables(K, es, pos_d, cf32, cf32_tk):
    nc, S = K.nc, K.S
    cosT = K.sb(es, "cosT", [128, T], F32)
    sinT = K.sb(es, "sinT", [128, T], F32)
    cos_tk, sin_tk = Tk("cos"), Tk("sin")
    with ExitStack() as es2:
        posi = K.sb(es2, "posi", [128, T], I32)
        ang = K.sb(es2, "ang", [128, T], F32)
        t1 = K.sb(es2, "ropet1", [128, T], F32)
        t2 = K.sb(es2, "ropet2", [128, T], F32)
        p_tk, a_tk, t1_tk, t2_tk = Tk(), Tk(), Tk(), Tk()
        S.dma("sp", posi[:], bass.AP(pos_d, 0, [[0, 128], [1, T]]), writes=[p_tk])
        S.op("dve", lambda e: e.tensor_copy(t1[:], posi[:]), reads=[p_tk], writes=[t1_tk])
        S.op("dve", lambda e: e.tensor_scalar(ang[:], t1[:], cf32[:, 0:1], None, ALU.mult), reads=[t1_tk, cf32_tk], writes=[a_tk])
        for (dst, dst_tk, shift) in ((sinT, sin_tk, 0.0), (cosT, cos_tk, np.pi / 2)):
            S.op("dve", lambda e, shift=shift: e.tensor_scalar(t1[:], ang[:], shift, 1.0 / TWO_PI, ALU.add, ALU.mult),
                 reads=[a_tk], writes=[t1_tk])
            S.op("dve", lambda e: e.tensor_scalar(t1[:], t1[:], MAGIC, -MAGIC, ALU.add, ALU.add), reads=[t1_tk], writes=[t1_tk])
            S.op("dve", lambda e: e.scalar_tensor_tensor(t2[:], t1[:], -TWO_PI, ang[:], ALU.mult, ALU.add),
                 reads=[t1_tk, a_tk], writes=[t2_tk])
            S.op("dve", lambda e, shift=shift: e.tensor_scalar(t2[:], t2[:], shift, 3.14159, ALU.add, ALU.min),
                 reads=[t2_tk], writes=[t2_tk])
            S.op("dve", lambda e: e.tensor_scalar(t2[:], t2[:], -3.14159, None, ALU.max), reads=[t2_tk], writes=[t2_tk])
            S.op("act", lambda e, dst=dst: e.activation(dst[:], t2[:], AF.Sin), reads=[t2_tk], writes=[dst_tk])
        S.op("dve", lambda e: e.tensor_scalar(sinT[:], sinT[:], cf32[:, 1:2], None, ALU.mult), reads=[sin_tk, cf32_tk], writes=[sin_tk])
        S.barrier()
    return cosT, cos_tk, sinT, sin_tk


def inproj_phase(K, C, l, hm, hm_tk, cosT, cos_tk, sinT, sin_tk, w_in, P, O, after_halo=None):
    nc, S = K.nc, K.S
    cb, cb_tk = C["cb"], C["cb_tk"]
    with ExitStack() as es:
        NWB = 3
        wbuf = [K.sb(es, f"wst{i}", [128, 8, 512], BF16) for i in range(NWB)]
        wb_tk = [Tk(f"wst{i}") for i in range(NWB)]
        st = {"nw": 0, "pr": 0, "ss": 0, "rot": 0, "cp": 0}

        def load_strip(col0, ncols):
            b = st["nw"] % NWB
            st["nw"] += 1
            S.dma("pool", wbuf[b][:, :, 0:ncols], w_in[l, :, col0:col0 + ncols].rearrange("(kc p) c -> p kc c", p=128),
                  writes=[wb_tk[b]])
            return wbuf[b], wb_tk[b]

        def proj_fm(wb, wtk, coff, M, tt):
            pb = st["pr"] % 4
            st["pr"] += 1
            for kc in range(8):
                S.op("pe", lambda e, kc=kc: e.matmul(K.ps[pb][0:M, :], wb[:, kc, coff:coff + M], hm[:, kc, tt * 512:tt * 512 + 512],
                                                     start=(kc == 0), stop=(kc == 7)), reads=[wtk, hm_tk], writes=[K.pst[pb]])
            return pb

        es1 = ExitStack()
        gh = K.sb(es1, "gh", [128, 4], F32)
        gh_tk = Tk("gh")
        for i, nm in enumerate(("a_q_norm", "a_k_norm", "b_q_norm", "b_k_norm")):
            for half in range(2):
                S.dma("sp", gh[half * 64:half * 64 + 64, i:i + 1], P[nm][l, :].rearrange("(p o) -> p o", o=1), writes=[gh_tk])
        NQ = 4
        sq = [K.sb(es1, f"qsq{i}", [128, 512], BF16) for i in range(NQ)]
        sq_tk = [Tk() for _ in range(NQ)]
        lnv = [K.sb(es1, f"qlnv{i}", [128, 512], F32) for i in range(NQ)]
        lnv_tk = [Tk() for _ in range(NQ)]
        rstd = [K.sb(es1, f"qrstd{i}", [128, 512], F32) for i in range(NQ)]
        rstd_tk = [Tk() for _ in range(NQ)]
        qn = [K.sb(es1, f"qn{i}", [128, 512], BF16) for i in range(NQ)]
        qn_tk = [Tk() for _ in range(NQ)]
        ta = [K.sb(es1, f"qta{i}", [128, 512], F32) for i in range(NQ)]
        ta_tk = [Tk() for _ in range(NQ)]
        tb = [K.sb(es1, f"qtb{i}", [128, 512], F32) for i in range(NQ)]
        tb_tk = [Tk() for _ in range(NQ)]
        stage = [K.sb(es1, f"qstage{i}", [128, T], BF16) for i in range(2)]
        stage_tk = [Tk(), Tk()]
        nqk = [0]

        def qk_block(wb, wtk, coff, gi, d, dst_ap, dst_tk, halo=None):
            sgi = nqk[0] % 2
            nqk[0] += 1
            stg, stg_tk = stage[sgi], stage_tk[sgi]
            TT = range(4)
            pbs = []
            for tt in TT:
                pb = tt
                pbs.append(pb)
                for kc in range(8):
                    S.op("pe", lambda e, kc=kc, pb=pb, tt=tt: e.matmul(K.ps[pb][:], wb[:, kc, coff:coff + 128], hm[:, kc, tt * 512:tt * 512 + 512],
                                                                   start=(kc == 0), stop=(kc == 7)), reads=[wtk, hm_tk], writes=[K.pst[pb]])
            for tt in TT:
                S.op("act", lambda e, tt=tt: e.activation(sq[tt][:], K.ps[pbs[tt]][:], AF.Square), reads=[K.pst[pbs[tt]]], writes=[sq_tk[tt]])
            for tt in TT:
                S.op("pe", lambda e, tt=tt: e.matmul(K.ps[4 + tt][:], cb[:, C_BONES:C_BONES + 128], sq[tt][:], start=True, stop=True),
                     reads=[cb_tk, sq_tk[tt]], writes=[K.pst[4 + tt]])
            for tt in TT:
                S.op("act", lambda e, tt=tt: e.activation(lnv[tt][:], K.ps[4 + tt][:], AF.Ln, bias=C["cf"][:, 0:1], scale=1.0 / 64),
                     reads=[K.pst[4 + tt], C["cf_tk"]], writes=[lnv_tk[tt]])
            for tt in TT:
                S.op("act", lambda e, tt=tt: e.activation(rstd[tt][:], lnv[tt][:], AF.Exp, scale=-0.5), reads=[lnv_tk[tt]], writes=[rstd_tk[tt]])
            for tt in TT:
                S.op("dve", lambda e, tt=tt: e.scalar_tensor_tensor(qn[tt][:], K.ps[pbs[tt]][:], gh[:, gi:gi + 1], rstd[tt][:], ALU.mult, ALU.mult),
                     reads=[K.pst[pbs[tt]], gh_tk, rstd_tk[tt]], writes=[qn_tk[tt]])
            for tt in TT:
                S.op("pe", lambda e, tt=tt: e.matmul(K.ps[4 + tt][:], cb[:, C_PERM:C_PERM + 128], qn[tt][:], start=True, stop=True),
                     reads=[cb_tk, qn_tk[tt]], writes=[K.pst[4 + tt]])
            for tt in TT:
                S.op("dve", lambda e, tt=tt: e.tensor_tensor(ta[tt][:], qn[tt][:], cosT[:, tt * 512:tt * 512 + 512], ALU.mult),
                     reads=[qn_tk[tt], cos_tk], writes=[ta_tk[tt]])
            for tt in TT:
                S.op("dve", lambda e, tt=tt: e.tensor_tensor(tb[tt][:], K.ps[4 + tt][:], sinT[:, tt * 512:tt * 512 + 512], ALU.mult),
                     reads=[K.pst[4 + tt], sin_tk], writes=[tb_tk[tt]])
            n = 512 // d
            for tt in TT:
                outv = stg[:].rearrange("p (r i) -> p r i", r=d)[:, :, tt * n:(tt + 1) * n]
                S.op("dve", lambda e, tt=tt, outv=outv: e.tensor_tensor(outv, ta[tt][:].rearrange("p (i r) -> p r i", r=d),
                                                                        tb[tt][:].rearrange("p (i r) -> p r i", r=d), ALU.add),
                     reads=[ta_tk[tt], tb_tk[tt]], writes=[stg_tk])
            S.dma("sp", dst_ap, stg[:], reads=[stg_tk], writes=[dst_tk])
            if halo is not None:
                nblk = T // (128 * d)
                src = stg[:].rearrange("p (r b i) -> p r b i", r=d, b=nblk)
                for (pk, lc, lw, so) in exp_segments(halo, 128 * d):
                    r0, r1 = so // 128, (so + lw) // 128
                    S.dma("sp", O["expH"][pk, :, lc:lc + lw].rearrange("p (r i) -> p r i", i=128), src[:, r0:r1, nblk - 1, :],
                          reads=[stg_tk], writes=[O["expH_tk"]])

        for which, co, gi, base in (("q", CO_AQ, 0, 0), ("k", CO_AK, 1, 12)):
            for sblk in range(3):
                wb, wtk = load_strip(co + sblk * 512, 512)
                d = A_DIL[sblk]
                for i in range(4):
                    blk = sblk * 4 + i
                    halo = None
                    if which == "k":
                        hc = NH_KA + i * KH_W + KH_OFF[sblk]
                        halo = hc
                    qk_block(wb, wtk, i * 128, gi, d, O["qkA"][base + blk], O["qkA_tk"], halo)
        wb, wtk = load_strip(CO_BQ, 512)
        for i in range(4):
            qk_block(wb, wtk, i * 128, 2, 1, O["qB"][i], O["qB_tk"])
        wb, wtk = load_strip(CO_BK, 256)
        qk_block(wb, wtk, 0, 3, 1, O["kB"][:, :], O["kB_tk"], NH_KB)

        vst = [K.sb(es1, f"vst{i}", [128, 512], BF16) for i in range(3)]
        vst_tk = [Tk(), Tk(), Tk()]
        nv = [0]

        def v_tm(wb, wtk, coff, ncols, tok_ap_fn, M, dst_list):
            pb = st["pr"] % 4
            st["pr"] += 1
            for kc in range(8):
                S.op("pe", lambda e, kc=kc: e.matmul(K.ps[pb][0:M, 0:ncols], tok_ap_fn(kc), wb[:, kc, coff:coff + ncols],
                                                     start=(kc == 0), stop=(kc == 7)), reads=[wtk, hm_tk], writes=[K.pst[pb]])
            b = nv[0] % 3
            nv[0] += 1
            if nv[0] % 2 == 0:
                S.op("act", lambda e: e.activation(vst[b][0:M, 0:ncols], K.ps[pb][0:M, 0:ncols], AF.Copy),
                     reads=[K.pst[pb]], writes=[vst_tk[b]])
            else:
                S.op("dve", lambda e: e.tensor_copy(vst[b][0:M, 0:ncols], K.ps[pb][0:M, 0:ncols]), reads=[K.pst[pb]], writes=[vst_tk[b]])
            for (dap, dtk) in dst_list:
                if isinstance(dap, tuple):
                    for (pk, lc, lw, so) in exp_segments(dap[0], dap[1]):
                        S.dma("sp", O["expH"][pk, :, lc:lc + lw], vst[b][0:M, so:so + lw], reads=[vst_tk[b]], writes=[dtk])
                else:
                    S.dma("sp", dap, vst[b][0:M, 0:ncols], reads=[vst_tk[b]], writes=[dtk])

        for blk in range(16):
            dl = [(O["vB"][blk], O["vB_tk"])]
            if blk == 15:
                dl.append(((NH_VB, 128), O["expH_tk"]))
            v_tm(wb, wtk, 128, 128, lambda kc, blk=blk: hm[:, kc, blk * 128:(blk + 1) * 128], 128, dl)
        for g in range(3):
            d = A_DIL[g]
            nblk = T // (128 * d)
            wb, wtk = load_strip(CO_AV + g * 512, 512)
            for r in range(d):
                for blk in range(nblk):
                    s0 = r + d * 128 * blk
                    dl = [(O["vA"][g, r * nblk + blk], O["vA_tk"])]
                    if blk == nblk - 1:
                        hc = NH_VA + (VH_OFF[g] + r) * 512
                        dl.append(((hc, 512), O["expH_tk"]))
                    v_tm(wb, wtk, 0, 512, lambda kc, s0=s0, d=d: hm[:, kc, s0:s0 + 127 * d + 1:d], 128, dl)

        S.barrier()
        es1.close()
        if after_halo is not None:
            after_halo()
        gla_prep(K, C, l, es, hm, hm_tk, w_in, P, O, load_strip, proj_fm, st)
        S.barrier()


def bc_last(ap, n):
    return bass.AP(ap.tensor, ap.offset, [list(x) for x in ap.ap] + [[0, n]])


def gla_prep(K, C, l, es, hm, hm_tk, w_in, P, O, load_strip, proj_fm, st):
    nc, S = K.nc, K.S
    cb, cb_tk = C["cb"], C["cb_tk"]
    gup = K.sb(es, "gup", [16, 256], BF16); gup_tk = Tk()
    S.dma("pool", gup[:], P["c_gate_up"][l, :, :], writes=[gup_tk])
    cgb = K.sb(es, "cgb", [64, 4], F32); cgb_tk = Tk()
    S.dma("sp", cgb[:], P["c_gate_bias"][l, :].rearrange("(h p) -> p h", p=64), writes=[cgb_tk], allow_slow_non_contiguous=True)
    S.op("dve", lambda e: e.tensor_scalar(cgb[:], cgb[:], -1.0, None, ALU.mult), reads=[cgb_tk], writes=[cgb_tk])
    rmask = K.sb(es, "rmask", [64, 512], F32); rm_tk = Tk()
    S.op("dve", lambda e: e.memset(rmask[:], 1.0), writes=[rm_tk])
    S.op("dve", lambda e: e.memset(rmask[:].rearrange("p (c i) -> p c i", i=64)[:, :, 0:1], 0.0), writes=[rm_tk])
    ebl = K.sb(es, "ebl", [64, 4, 32], F32); ebl_tk = Tk()
    nbl = K.sb(es, "nbl", [64, 4, 32], F32); nbl_tk = Tk()
    ebl2 = K.sb(es, "ebl2", [128, 2, 32], F32); ebl2_tk = Tk()
    kd_all = K.sb(es, "kd_all", [64, 32, 256], BF16); kd_tk = Tk()
    cv_all = K.sb(es, "cv_all", [64, 32, 512], BF16); cv_tk = Tk()
    qes = [K.sb(es, f"qes{p}", [128, T], BF16) for p in range(2)]
    kes = [K.sb(es, f"kes{p}", [128, T], BF16) for p in range(2)]
    qes_tk = [Tk() for _ in range(2)]
    kes_tk = [Tk() for _ in range(2)]
    glow_sb = K.sb(es, "glow_sb", [16, 512], BF16); glow_tk = Tk()
    tmp1 = [K.sb(es, f"gtmp1{p}", [128, 512], F32) for p in range(2)]; tmp1_tk = [Tk(), Tk()]
    tmp2 = [K.sb(es, f"gtmp2{p}", [128, 512], F32) for p in range(2)]; tmp2_tk = [Tk(), Tk()]
    nbuf = [K.sb(es, f"gnbuf{p}", [128, 512], F32) for p in range(2)]; nbuf_tk = [Tk(), Tk()]
    ebuf = [K.sb(es, f"gebuf{p}", [128, 512], F32) for p in range(2)]; ebuf_tk = [Tk(), Tk()]
    enbuf = [K.sb(es, f"genbuf{p}", [128, 512], F32) for p in range(2)]; enbuf_tk = [Tk(), Tk()]
    kdT = [K.sb(es, f"gkdT{p}", [128, 512], BF16) for p in range(2)]; kdT_tk = [Tk(), Tk()]
    cgb2 = K.sb(es, "cgb2", [128, 2], F32); cgb2_tk = Tk()
    S.dma("sp", cgb2[:], P["c_gate_bias"][l, :].rearrange("(q p) -> p q", p=128), writes=[cgb2_tk], allow_slow_non_contiguous=True)
    S.op("dve", lambda e: e.tensor_scalar(cgb2[:], cgb2[:], -1.0, None, ALU.mult), reads=[cgb2_tk], writes=[cgb2_tk])
    rmask2 = K.sb(es, "rmask2", [128, 512], F32); rm2_tk = Tk()
    S.op("dve", lambda e: e.memset(rmask2[:], 1.0), writes=[rm2_tk])
    S.op("dve", lambda e: e.memset(rmask2[:].rearrange("p (c i) -> p c i", i=64)[:, :, 0:1], 0.0), writes=[rm2_tk])
    psT = [K.ps[6][:].bitcast(BF16), K.ps[7][:].bitcast(BF16)]

    wq, wq_tk = load_strip(CO_CQ, 512)
    wgl, wgl_tk = load_strip(CO_GL, 16)
    PR = range(2)
    for tt in range(4):
        ts = slice(tt * 512, tt * 512 + 512)
        pbg = proj_fm(wgl, wgl_tk, 0, 16, tt)
        S.op("act", lambda e: e.activation(glow_sb[:], K.ps[pbg][0:16, :], AF.Copy), reads=[K.pst[pbg]], writes=[glow_tk])
        pbs = []
        for p in PR:
            pb = st["pr"] % 4
            st["pr"] += 1
            pbs.append(pb)
            S.op("pe", lambda e, p=p, pb=pb: e.matmul(K.ps[pb][:], gup[:, 128 * p:128 * p + 128], glow_sb[:], start=True, stop=True),
                 reads=[gup_tk, glow_tk], writes=[K.pst[pb]], sig="sync")
        for p in PR:
            S.op("act", lambda e, p=p: e.activation(tmp1[p][:], K.ps[pbs[p]][:], AF.Exp, bias=cgb2[:, p:p + 1], scale=-1.0),
                 reads=[K.pst[pbs[p]], cgb2_tk], writes=[tmp1_tk[p]])
        for p in PR:
            S.op("act", lambda e, p=p: e.activation(tmp2[p][:], tmp1[p][:], AF.Ln, bias=C["cf"][:, 1:2], scale=1.0),
                 reads=[tmp1_tk[p], C["cf_tk"]], writes=[tmp2_tk[p]])
        for p in PR:
            S.op("dve", lambda e, p=p: e.tensor_tensor_scan(nbuf[p][:], rmask2[:], tmp2[p][:], 0.0, ALU.mult, ALU.add),
                 reads=[rm2_tk, tmp2_tk[p]], writes=[nbuf_tk[p]])
        for p in PR:
            S.op("act", lambda e, p=p: e.activation(ebuf[p][:], nbuf[p][:], AF.Exp, scale=-1.0 / 16), reads=[nbuf_tk[p]], writes=[ebuf_tk[p]])
        for p in PR:
            S.op("act", lambda e, p=p: e.activation(enbuf[p][:], nbuf[p][:], AF.Exp, scale=1.0 / 16), reads=[nbuf_tk[p]], writes=[enbuf_tk[p]])
        pqs = [proj_fm(wq, wq_tk, 128 * p, 128, tt) for p in PR]
        for p in PR:
            S.op("dve", lambda e, p=p: e.scalar_tensor_tensor(qes[p][:, ts], K.ps[pqs[p]][:], 0.125, ebuf[p][:], ALU.mult, ALU.mult),
                 reads=[K.pst[pqs[p]], ebuf_tk[p]], writes=[qes_tk[p]])
        pks = [proj_fm(wq, wq_tk, 256 + 128 * p, 128, tt) for p in PR]
        for p in PR:
            S.op("dve", lambda e, p=p: e.tensor_tensor(kes[p][:, ts], K.ps[pks[p]][:], enbuf[p][:], ALU.mult),
                 reads=[K.pst[pks[p]], enbuf_tk[p]], writes=[kes_tk[p]])
        for p in PR:
            S.op("dve", lambda e, p=p: e.tensor_copy(ebl2[:, p, 8 * tt:8 * tt + 8], ebuf[p][:, 63::64]), reads=[ebuf_tk[p]], writes=[ebl2_tk])
            for hh in range(2):
                S.op("dve", lambda e, p=p, hh=hh: e.tensor_copy(ebl[:, 2 * p + hh, 8 * tt:8 * tt + 8], ebuf[p][64 * hh:64 * hh + 64, 63::64]),
                     reads=[ebuf_tk[p]], writes=[ebl_tk])
                S.op("dve", lambda e, p=p, hh=hh: e.tensor_copy(nbl[:, 2 * p + hh, 8 * tt:8 * tt + 8], nbuf[p][64 * hh:64 * hh + 64, 63::64]),
                     reads=[nbuf_tk[p]], writes=[nbl_tk])
        for p in PR:
            S.op("dve", lambda e, p=p: e.tensor_tensor(kdT[p][:].rearrange("p (c i) -> p c i", i=64),
                                                       kes[p][:, ts].rearrange("p (c i) -> p c i", i=64),
                                                       bc_last(ebl2[:, p, 8 * tt:8 * tt + 8], 64), ALU.mult),
                 reads=[kes_tk[p], ebl2_tk], writes=[kdT_tk[p]])
        for p in PR:
            for c in range(8):
                S.op("pe", lambda e, c=c, p=p: e.transpose(psT[p][0:64, c * 128:(c + 1) * 128], kdT[p][:, c * 64:(c + 1) * 64], cb[:, C_IDENT:C_IDENT + 128]),
                     reads=[kdT_tk[p], cb_tk], writes=[K.pst[6 + p]])
            S.op("act", lambda e, p=p: e.activation(kd_all[:, 8 * tt:8 * tt + 8, 128 * p:128 * p + 128],
                                                    psT[p][0:64, 0:1024].rearrange("q (c i) -> q c i", i=128), AF.Copy),
                 reads=[K.pst[6 + p]], writes=[kd_tk])
    for h in range(4):
        p, hh = h // 2, h % 2
        S.dma("sp", O["qeT"][h], qes[p][64 * hh:64 * hh + 64, :], reads=[qes_tk[p]], writes=[O["qeT_tk"]])
        S.dma("sp", O["keT"][h], kes[p][64 * hh:64 * hh + 64, :], reads=[kes_tk[p]], writes=[O["keT_tk"]])
    S.dma("sp", O["kd"][:, :, :], kd_all[:], reads=[kd_tk], writes=[O["kd_tk"]])
    S.dma("sp", O["ebl"][:, :], ebl[:].rearrange("p h c -> p (h c)"), reads=[ebl_tk], writes=[O["ebl_tk"]])

    wv, wv_tk = load_strip(CO_CV, 512)
    for c in range(32):
        pb = st["pr"] % 4
        st["pr"] += 1
        for kc in range(8):
            S.op("pe", lambda e, kc=kc: e.matmul(K.ps[pb][0:64, :], hm[:, kc, c * 64:(c + 1) * 64], wv[:, kc, :],
                                                 start=(kc == 0), stop=(kc == 7)), reads=[wv_tk, hm_tk], writes=[K.pst[pb]])
        if c % 2 == 0:
            S.op("act", lambda e: e.activation(cv_all[:, c, :], K.ps[pb][0:64, :], AF.Copy), reads=[K.pst[pb]], writes=[cv_tk])
        else:
            S.op("dve", lambda e: e.tensor_copy(cv_all[:, c, :], K.ps[pb][0:64, :]), reads=[K.pst[pb]], writes=[cv_tk])
    S.dma("sp", O["cv"][:, :, :], cv_all[:], reads=[cv_tk], writes=[O["cv_tk"]])

    wr, wr_tk = load_strip(CO_CR, 512)
    crs = [K.sb(es, f"crs{i}", [128, T], BF16) for i in range(2)]
    crs_tk = [Tk(), Tk()]
    for h in range(4):
        for tt in range(4):
            pb = proj_fm(wr, wr_tk, 128 * h, 128, tt)
            S.op("act", lambda e: e.activation(crs[h % 2][:, tt * 512:tt * 512 + 512], K.ps[pb][:], AF.Silu),
                 reads=[K.pst[pb]], writes=[crs_tk[h % 2]])
        S.dma("sp", O["crT"][h], crs[h % 2][:], reads=[crs_tk[h % 2]], writes=[O["crT_tk"]])

    Sst = K.sb(es, "Sst", [64, 4, 128], F32); Sst_tk = Tk()
    S.op("dve", lambda e: e.memset(Sst[:], 0.0), writes=[Sst_tk])
    for c in range(32):
        pb = st["pr"] % 4
        st["pr"] += 1
        for h in range(4):
            S.op("pe", lambda e, h=h: e.matmul(K.ps[pb][0:64, 128 * h:128 * h + 128], kd_all[:, c, 64 * h:64 * h + 64],
                                               cv_all[:, c, 128 * h:128 * h + 128], start=True, stop=True),
                 reads=[kd_tk, cv_tk], writes=[K.pst[pb]], sig="sync")
        S.op("dve", lambda e: e.tensor_tensor(Sst[:], Sst[:], bc_last(ebl[:, :, c], 128), ALU.mult), reads=[ebl_tk, Sst_tk], writes=[Sst_tk])
        S.op("dve", lambda e: e.tensor_tensor(Sst[:], Sst[:], K.ps[pb][0:64, :].rearrange("p (h e) -> p h e", h=4), ALU.add),
             reads=[K.pst[pb], Sst_tk], writes=[Sst_tk])
    gst = K.sb(es, "gst", [64, 4, 129], F32); gst_tk = Tk()
    dsum = K.sb(es, "dsum", [64, 4], F32); dsum_tk = Tk()
    S.op("dve", lambda e: e.tensor_reduce(dsum[:], nbl[:], mybir.AxisListType.X, ALU.add), reads=[nbl_tk], writes=[dsum_tk])
    S.op("act", lambda e: e.activation(gst[:, :, 128], dsum[:], AF.Exp, scale=-1.0 / 16), reads=[dsum_tk], writes=[gst_tk])
    S.op("dve", lambda e: e.tensor_copy(gst[:, :, 0:128], Sst[:]), reads=[Sst_tk], writes=[gst_tk])
    S.dma("sp", O["expG"][:, :], gst[:].rearrange("p h e -> p (h e)"), reads=[gst_tk], writes=[O["expG_tk"]])


def attn_phase(K, C, l, I, P, yT, yT_tk):
    nc, S = K.nc, K.S
    cb, cb_tk = C["cb"], C["cb_tk"]
    with ExitStack() as es:
        qbuf = [[K.sb(es, f"qbuf{i}_{hh}", [128, T], BF16) for hh in range(2)] for i in range(2)]
        kbuf = [K.sb(es, f"kbuf{i}", [128, 2 * T], BF16) for i in range(2)]
        vbuf = [K.sb(es, f"vbuf{i}", [128, 32, 128], BF16) for i in range(2)]
        q_tk = [Tk(), Tk()]; k_tk = [Tk(), Tk()]; v_tk = [Tk(), Tk()]
        for i in range(2):
            for hh in range(2):
                z0 = 64 * (1 - hh)
                S.op("dve", lambda e, i=i, hh=hh, z0=z0: e.memset(qbuf[i][hh][z0:z0 + 64, :], 0.0), writes=[q_tk[i]])
        acc = K.sb(es, "acc", [64, 4, T], F32); acc_tk = Tk()
        pT = [K.sb(es, f"pT{i}", [128, 512], BF16) for i in range(2)]
        pT_tk = [Tk(), Tk()]
        m0 = K.sb(es, "m0", [128, 256], BF16); m0_tk = Tk()
        S.dma("pool", m0[:], I["m0"][:, :], writes=[m0_tk])
        esink = K.sb(es, "esink", [64, 8], F32); es_tk = Tk()
        S.dma("sp", esink[:], bass.AP(P["b_sinks"].tensor if hasattr(P["b_sinks"], "tensor") else P["b_sinks"], l * 8, [[0, 64], [1, 8]]),
              writes=[es_tk])
        S.op("act", lambda e: e.activation(esink[:], esink[:], AF.Exp), reads=[es_tk], writes=[es_tk])
        lden = K.sb(es, "lden", [64, T], F32); lden_tk = Tk()
        mk4 = K.sb(es, "mk4", [128, 4, 512], BF16); mk4_tk = Tk()
        for vi, (mcol, m0o) in enumerate(((C_MASKA, 0), (C_MASKB, 128))):
            for hh in range(2):
                S.op("dve", lambda e, vi=vi, hh=hh, mcol=mcol: e.tensor_copy(mk4[:, 2 * vi, hh * 256:hh * 256 + 256], cb[:, mcol:mcol + 256]),
                     reads=[cb_tk], writes=[mk4_tk])
                S.op("dve", lambda e, vi=vi, hh=hh, mcol=mcol: e.tensor_copy(mk4[:, 2 * vi + 1, hh * 256 + 128:hh * 256 + 256], cb[:, mcol + 128:mcol + 256]),
                     reads=[cb_tk], writes=[mk4_tk])
                S.op("dve", lambda e, vi=vi, hh=hh, m0o=m0o: e.tensor_copy(mk4[:, 2 * vi + 1, hh * 256:hh * 256 + 128], m0[:, m0o:m0o + 128]),
                     reads=[m0_tk], writes=[mk4_tk])
        nu = [0]
        npq = [0]

        def unit(branch, g, hp, first_group):
            d = A_DIL[g] if branch == "A" else 1
            nblk = T // (128 * d)
            W = (nblk + 1) * 128
            st = {}

            def setup():
                bi = nu[0] % 2
                nu[0] += 1
                qb, kb, vb = qbuf[bi], kbuf[bi], vbuf[bi]
                kv = kb[:, 0:d * W].rearrange("p (r w) -> p r w", r=d)
                qv = [qb[hh][:].rearrange("p (r w) -> p r w", r=d) for hh in range(2)]
                vv = vb[:, 0:d * (nblk + 1), :].rearrange("p (r b) c -> p r b c", r=d)
                if branch == "A":
                    blk = g * 4 + hp
                    for hh in range(2):
                        S.dma("sp", qb[hh][64 * hh:64 * hh + 64, :], I["qkA"][blk, 64 * hh:64 * hh + 64, :], reads=[I["qkA_tk"]], writes=[q_tk[bi]])
                    S.dma("sp", kv[:, :, 128:], I["qkA"][12 + blk].rearrange("p (r w) -> p r w", r=d), reads=[I["qkA_tk"]], writes=[k_tk[bi]])
                    hc = NH_KA + hp * KH_W + KH_OFF[g]
                    S.dma("sp", kv[:, :, 0:128], I["haloH"][:, hc:hc + 128 * d].rearrange("p (r i) -> p r i", r=d),
                          reads=[I["haloH_tk"]], writes=[k_tk[bi]])
                    for r in range(d):
                        S.dma("sp", vv[:, r, 1:, :], I["vA"][g, r * nblk:(r + 1) * nblk, :, hp * 128:(hp + 1) * 128].rearrange("b p c -> p b c"),
                              reads=[I["vA_tk"]], writes=[v_tk[bi]])
                    hv = NH_VA + VH_OFF[g] * 512
                    S.dma("sp", vv[:, :, 0, :], I["haloH"][:, hv:hv + d * 512].rearrange("p (r c) -> p r c", r=d)[:, :, hp * 128:(hp + 1) * 128],
                          reads=[I["haloH_tk"]], writes=[v_tk[bi]])
                    st["vcol"] = lambda hh: slice(hh * 64, hh * 64 + 64)
                else:
                    kvh = hp // 2
                    for hh in range(2):
                        S.dma("sp", qb[hh][64 * hh:64 * hh + 64, :], I["qB"][hp, 64 * hh:64 * hh + 64, :], reads=[I["qB_tk"]], writes=[q_tk[bi]])
                    for half in range(2):
                        S.dma("sp", kv[half * 64:half * 64 + 64, :, 128:], I["kB"][kvh * 64:kvh * 64 + 64, :].rearrange("p (r w) -> p r w", r=1),
                              reads=[I["kB_tk"]], writes=[k_tk[bi]])
                        S.dma("sp", kv[half * 64:half * 64 + 64, :, 0:128],
                              I["haloH"][kvh * 64:kvh * 64 + 64, NH_KB:NH_KB + 128].rearrange("p (r i) -> p r i", r=1),
                              reads=[I["haloH_tk"]], writes=[k_tk[bi]])
                    S.dma("sp", vv[:, 0, 1:, :], I["vB"][:, :, :].rearrange("b p c -> p b c"), reads=[I["vB_tk"]], writes=[v_tk[bi]])
                    S.dma("sp", vv[:, 0, 0, :], I["haloH"][:, NH_VB:NH_VB + 128], reads=[I["haloH_tk"]], writes=[v_tk[bi]])
                    st["vcol"] = lambda hh: slice(kvh * 64, kvh * 64 + 64)
                st.update(bi=bi, kv=kv, qv=qv, vv=vv)

            items = []
            for r in range(d):
                for b in range(nblk):
                    it = {}

                    def front(r=r, b=b, it=it):
                        if not st:
                            setup()
                        bi, kv, qv = st["bi"], st["kv"], st["qv"]
                        i2 = npq[0] % 2
                        npq[0] += 1
                        it["i2"] = i2
                        pss = 2 * i2
                        for hh in range(2):
                            bp = 64 * hh
                            for half in range(2):
                                o_ap = K.ps[pss][:, (hh * 2 + half) * 128:(hh * 2 + half + 1) * 128]
                                S.op("pe", lambda e, o_ap=o_ap, hh=hh, half=half: e.matmul(
                                    o_ap, kv[:, r, (b + half) * 128:(b + half + 1) * 128], qv[hh][:, r, b * 128:(b + 1) * 128],
                                    start=True, stop=True), reads=[k_tk[bi], q_tk[bi]], writes=[K.pst[pss]])
                        S.op("act", lambda e: e.activation(pT[i2][:], K.ps[pss][:], AF.Exp, scale=0.125), reads=[K.pst[pss]], writes=[pT_tk[i2]])
                        mv = (0 if branch == "A" else 2) + (1 if b == 0 else 0)
                        S.op(MASK_ENG, lambda e, mv=mv: e.tensor_tensor(pT[i2][:], pT[i2][:], mk4[:, mv, :], ALU.mult),
                             reads=[pT_tk[i2], mk4_tk], writes=[pT_tk[i2]])

                    def back(r=r, b=b, it=it):
                        bi, vv, vcol = st["bi"], st["vv"], st["vcol"]
                        i2 = it["i2"]
                        pso = 2 * i2 + 1
                        for hh in range(2):
                            o_ap = K.ps[pso][0:64, hh * 128:(hh + 1) * 128]
                            for half in range(2):
                                S.op("pe", lambda e, o_ap=o_ap, hh=hh, half=half: e.matmul(
                                    o_ap, vv[:, r, b + half, vcol(hh)], pT[i2][:, (hh * 2 + half) * 128:(hh * 2 + half + 1) * 128],
                                    start=(half == 0), stop=(half == 1)), reads=[v_tk[bi], pT_tk[i2]], writes=[K.pst[pso]])
                        for half in range(2):
                            S.op("pe", lambda e, half=half: e.matmul(
                                K.ps[pso][0:64, 256:512].rearrange("p (h q) -> p h q", h=2), C["ones"][:, 0:64],
                                pT[i2][:].rearrange("p (h f q) -> p h f q", h=2, f=2)[:, :, half, :],
                                start=(half == 0), stop=(half == 1)), reads=[pT_tk[i2], C["ones_tk"]], writes=[K.pst[pso]])
                        av = acc[:].rearrange("p n (i r) -> p n r i", r=d)[:, :, r, b * 128:(b + 1) * 128]
                        pv = K.ps[pso][0:64, :].rearrange("p (n i) -> p n i", n=4)
                        if first_group:
                            S.op("act", lambda e: e.activation(av, pv, AF.Copy), reads=[K.pst[pso]], writes=[acc_tk])
                        else:
                            S.op("dve", lambda e: e.tensor_tensor(av, av, pv, ALU.add), reads=[K.pst[pso], acc_tk], writes=[acc_tk])

                    items.append([front, back, None])
            return items

        def finalize(branch, hp):
            for hh in range(2):
                den = acc[:, 2 + hh, :]
                if branch == "A":
                    S.op("act", lambda e: e.activation(lden[:], den, AF.Ln), reads=[acc_tk], writes=[lden_tk])
                else:
                    hq = 2 * hp + hh
                    S.op("act", lambda e: e.activation(lden[:], den, AF.Ln, bias=esink[:, hq:hq + 1], scale=1.0),
                         reads=[acc_tk, es_tk], writes=[lden_tk])
                S.op("act", lambda e: e.activation(lden[:], lden[:], AF.Exp, scale=-1.0), reads=[lden_tk], writes=[lden_tk])
                yb = hp if branch == "A" else 4 + hp
                S.op("dve", lambda e: e.tensor_tensor(yT[64 * hh:64 * hh + 64, yb, :], acc[:, hh, :], lden[:], ALU.mult),
                     reads=[acc_tk, lden_tk], writes=[yT_tk])

        items = []
        for hp in range(4):
            for g in range(3):
                items += unit("A", g, hp, g == 0)
            items[-1][2] = ("A", hp)
        for hp in range(4):
            items += unit("B", 0, hp, True)
            items[-1][2] = ("B", hp)
        prev = None
        for it in items:
            it[0]()
            if prev is not None:
                prev[1]()
                if prev[2] is not None:
                    finalize(*prev[2])
            prev = it
        prev[1]()
        finalize(*prev[2])
        S.barrier()


def gla_phase(K, C, l, I, P, yT, yT_tk):
    nc, S = K.nc, K.S
    cb, cb_tk = C["cb"], C["cb_tk"]
    with ExitStack() as es:
        qe = K.sb(es, "gqe", [128, 4, T], BF16); qe_tk = Tk()
        ke = K.sb(es, "gke", [128, 4, T], BF16); ke_tk = Tk()
        cv = K.sb(es, "gcv", [128, 32, 512], BF16); cv_tk = Tk()
        kd = K.sb(es, "gkd", [128, 32, 256], BF16); kd_tk = Tk()
        zc = C["cf"][64:128, 2:3]
        def zfill(eng, t, tk, n):
            v = t[64:128, :, :].rearrange("p a b -> p (a b)") if len(t.shape) == 3 else t[64:128, :]
            if eng == "act":
                S.op("act", lambda e: e.activation(v, bass.AP(zc.tensor, zc.offset, [list(zc.ap[0]), [0, n]]), AF.Copy), reads=[C["cf_tk"]], writes=[tk])
            else:
                S.op("dve", lambda e: e.memset(v, 0.0), writes=[tk])
        zfill("dve", qe, qe_tk, 4 * T)
        zfill("dve", ke, ke_tk, 4 * T)
        zfill("act", cv, cv_tk, 32 * 512)
        zfill("dve", kd, kd_tk, 32 * 256)
        ebl = K.sb(es, "gebl", [64, 4, 32], F32); ebl_tk = Tk()
        gall = K.sb(es, "gall", [64, 4, NG], F32); gall_tk = Tk()
        gm = K.sb(es, "gm", [64, 8], F32); gm_tk = Tk()
        Sall = K.sb(es, "Sall", [128, 32, 512], BF16); Sall_tk = Tk()
        zfill("act", Sall, Sall_tk, 32 * 512)
        Sst = K.sb(es, "Sst2", [64, 4, 128], F32); Sst_tk = Tk()
        acf = K.sb(es, "acf", [64, 4], F32); acf_tk = Tk()
        gco = K.sb(es, "gco", [128, 1], F32); gco_tk = Tk()
        for h in range(4):
            S.dma("sp", qe[0:64, h, :], I["qeT"][h], reads=[I["qeT_tk"]], writes=[qe_tk])
            S.dma("sp", ke[0:64, h, :], I["keT"][h], reads=[I["keT_tk"]], writes=[ke_tk])
        S.dma("sp", cv[0:64], I["cv"][:, :, :], reads=[I["cv_tk"]], writes=[cv_tk])
        S.dma("sp", kd[0:64], I["kd"][:, :, :], reads=[I["kd_tk"]], writes=[kd_tk])
        S.dma("sp", ebl[:].rearrange("p h c -> p (h c)"), I["ebl"][:, :], reads=[I["ebl_tk"]], writes=[ebl_tk])
        S.dma("sp", gall[:], I["allG"][:, :].rearrange("(r p) c -> p r c", p=64), reads=[I["allG_tk"]], writes=[gall_tk])
        S.dma("sp", gm[:], I["gmask"][:, :], writes=[gm_tk])
        S.dma("sp", gco[:], P["c_out_norm"][l, :].rearrange("(p o) -> p o", o=1), writes=[gco_tk])
        S.op("dve", lambda e: e.memset(Sst[:], 0.0), writes=[Sst_tk])
        for r in range(4):
            gv = gall[:, r, :].rearrange("p (h e) -> p h e", h=4)
            S.op("dve", lambda e, gv=gv, r=r: e.tensor_scalar(acf[:], gv[:, :, 128], gm[:, r:r + 1], gm[:, 4 + r:5 + r], ALU.mult, ALU.add),
                 reads=[gall_tk, gm_tk], writes=[acf_tk])
            S.op("dve", lambda e: e.tensor_tensor(Sst[:], Sst[:], bc_last(acf[:, :], 128), ALU.mult), reads=[acf_tk, Sst_tk], writes=[Sst_tk])
            S.op("dve", lambda e, gv=gv, r=r: e.scalar_tensor_tensor(Sst[:], gv[:, :, 0:128], gm[:, r:r + 1], Sst[:], ALU.mult, ALU.add),
                 reads=[gall_tk, gm_tk, Sst_tk], writes=[Sst_tk])
        for c in range(32):
            pb = 6 + (c % 2)
            S.op("act", lambda e: e.activation(Sall[0:64, c, :], Sst[:].rearrange("p h e -> p (h e)"), AF.Copy), reads=[Sst_tk], writes=[Sall_tk])
            for h in range(4):
                S.op("pe", lambda e, h=h: e.matmul(K.ps[pb][0:64, 128 * h:128 * h + 128], kd[:, c, 64 * h:64 * h + 64],
                                                   cv[:, c, 128 * h:128 * h + 128], start=True, stop=True),
                     reads=[kd_tk, cv_tk], writes=[K.pst[pb]])
            S.op("dve", lambda e: e.tensor_tensor(Sst[:], Sst[:], bc_last(ebl[:, :, c], 128), ALU.mult), reads=[ebl_tk, Sst_tk], writes=[Sst_tk])
            S.op("dve", lambda e: e.tensor_tensor(Sst[:], Sst[:], K.ps[pb][0:64, :].rearrange("p (h e) -> p h e", h=4), ALU.add),
                 reads=[K.pst[pb], Sst_tk], writes=[Sst_tk])
        crb = [K.sb(es, f"crb{i}", [128, T], BF16) for i in range(2)]
        crb_tk = [Tk(), Tk()]
        attm = [K.sb(es, f"attm{i}", [128, 512], BF16) for i in range(2)]
        attm_tk = [Tk(), Tk()]
        for i in range(2):
            zfill("dve", attm[i], attm_tk[i], 512)
        osq = [K.sb(es, f"osq{i}", [128, 512], BF16) for i in range(2)]
        osq_tk = [Tk(), Tk()]
        oln = [K.sb(es, f"oln{i}", [128, 512], F32) for i in range(2)]
        oln_tk = [Tk(), Tk()]
        ors = [K.sb(es, f"ors{i}", [128, 512], F32) for i in range(2)]
        ors_tk = [Tk(), Tk()]
        oy = [K.sb(es, f"oy{i}", [128, 512], F32) for i in range(2)]
        oy_tk = [Tk(), Tk()]
        tril = cb[0:64, C_TRIL:C_TRIL + 64]
        tril_b = bass.AP(tril.tensor, tril.offset, [list(tril.ap[0]), [0, 8], list(tril.ap[1])])
        n = 0
        for h in range(4):
            S.dma("sp", crb[h % 2][:], I["crT"][h], reads=[I["crT_tk"]], writes=[crb_tk[h % 2]])
            for tt in range(4):
                i2 = n % 2
                n += 1
                pa, po, pss = i2, 2 + i2, 4 + i2
                for c8 in range(8):
                    c = tt * 8 + c8
                    cs = slice(c * 64, c * 64 + 64)
                    S.op("pe", lambda e, c8=c8, cs=cs: e.matmul(K.ps[pa][0:64, c8 * 64:c8 * 64 + 64], ke[:, h, cs], qe[:, h, cs], start=True, stop=True),
                         reads=[ke_tk, qe_tk], writes=[K.pst[pa]])
                S.op("dve", lambda e: e.tensor_tensor(attm[i2][0:64, :].rearrange("p (c i) -> p c i", i=64),
                                                      K.ps[pa][0:64, :].rearrange("p (c i) -> p c i", i=64), tril_b, ALU.mult),
                     reads=[K.pst[pa], cb_tk], writes=[attm_tk[i2]])
                for c8 in range(8):
                    c = tt * 8 + c8
                    cs = slice(c * 64, c * 64 + 64)
                    o_ap = K.ps[po][:, c8 * 64:c8 * 64 + 64]
                    S.op("pe", lambda e, o_ap=o_ap, c=c, c8=c8: e.matmul(o_ap, cv[:, c, 128 * h:128 * h + 128], attm[i2][:, c8 * 64:c8 * 64 + 64],
                                                                         start=True, stop=False), reads=[cv_tk, attm_tk[i2]], writes=[K.pst[po]])
                    S.op("pe", lambda e, o_ap=o_ap, c=c, cs=cs: e.matmul(o_ap, Sall[:, c, 128 * h:128 * h + 128], qe[:, h, cs], start=False, stop=True),
                         reads=[Sall_tk, qe_tk], writes=[K.pst[po]])
                S.op("act", lambda e: e.activation(osq[i2][:], K.ps[po][:], AF.Square), reads=[K.pst[po]], writes=[osq_tk[i2]])
                S.op("pe", lambda e: e.matmul(K.ps[pss][:], C["ones"][:], osq[i2][:], start=True, stop=True),
                     reads=[C["ones_tk"], osq_tk[i2]], writes=[K.pst[pss]])
                S.op("act", lambda e: e.activation(oln[i2][:], K.ps[pss][:], AF.Ln, bias=C["cf"][:, 0:1], scale=1.0 / 128),
                     reads=[K.pst[pss], C["cf_tk"]], writes=[oln_tk[i2]])
                S.op("act", lambda e: e.activation(ors[i2][:], oln[i2][:], AF.Exp, scale=-0.5), reads=[oln_tk[i2]], writes=[ors_tk[i2]])
                S.op("dve", lambda e: e.scalar_tensor_tensor(oy[i2][:], K.ps[po][:], gco[:, 0:1], ors[i2][:], ALU.mult, ALU.mult),
                     reads=[K.pst[po], gco_tk, ors_tk[i2]], writes=[oy_tk[i2]])
                S.op("dve", lambda e: e.tensor_tensor(yT[:, 8 + h, tt * 512:tt * 512 + 512], oy[i2][:], crb[h % 2][:, tt * 512:tt * 512 + 512], ALU.mult),
                     reads=[oy_tk[i2], crb_tk[h % 2]], writes=[yT_tk])
        S.barrier()


def merge_phase(K, C, l, hm, hm_tk, yT, yT_tk, w_in, w_branch, mT_d, mT_d_tk):
    nc, S = K.nc, K.S
    with ExitStack() as es:
        wg = [[K.sb(es, f"wg{b}_{i}", [128, 8, 128], BF16) for i in range(3)] for b in range(2)]
        wbr = [[K.sb(es, f"wbr{b}_{i}", [128, 4, 128], BF16) for i in range(3)] for b in range(2)]
        w_tk = [Tk(), Tk()]
        sgt = [K.sb(es, f"sgt{i}", [128, 512], F32) for i in range(3)]
        sgt_tk = [Tk() for _ in range(3)]
        m1 = K.sb(es, "mm1", [128, 512], F32); m1_tk = Tk()
        m2 = K.sb(es, "mm2", [128, 512], F32); m2_tk = Tk()
        mst = [K.sb(es, f"mst{i}", [128, T], BF16) for i in range(2)]
        mst_tk = [Tk(), Tk()]
        for db in range(8):
            b = db % 2
            for i in range(3):
                c0 = CO_GATE + i * 1024 + db * 128
                S.dma("pool", wg[b][i][:], w_in[l, :, c0:c0 + 128].rearrange("(kc p) c -> p kc c", p=128), writes=[w_tk[b]])
                S.dma("pool", wbr[b][i][:], w_branch[l, i, :, db * 128:(db + 1) * 128].rearrange("(kc p) c -> p kc c", p=128), writes=[w_tk[b]])
            for tt in range(4):
                ts = slice(tt * 512, tt * 512 + 512)
                for i in range(3):
                    for kc in range(8):
                        S.op("pe", lambda e, i=i, kc=kc: e.matmul(K.ps[i][:], wg[b][i][:, kc, :], hm[:, kc, ts], start=(kc == 0), stop=(kc == 7)),
                             reads=[w_tk[b], hm_tk], writes=[K.pst[i]])
                    S.op("act", lambda e, i=i: e.activation(sgt[i][:], K.ps[i][:], AF.Sigmoid), reads=[K.pst[i]], writes=[sgt_tk[i]])
                for i in range(3):
                    for kc in range(4):
                        S.op("pe", lambda e, i=i, kc=kc: e.matmul(K.ps[3 + i][:], wbr[b][i][:, kc, :], yT[:, 4 * i + kc, ts], start=(kc == 0), stop=(kc == 3)),
                             reads=[w_tk[b], yT_tk], writes=[K.pst[3 + i]])
                S.op("dve", lambda e: e.tensor_tensor(m1[:], sgt[0][:], K.ps[3][:], ALU.mult), reads=[sgt_tk[0], K.pst[3]], writes=[m1_tk])
                S.op("dve", lambda e: e.tensor_tensor(m2[:], sgt[1][:], K.ps[4][:], ALU.mult), reads=[sgt_tk[1], K.pst[4]], writes=[m2_tk])
                S.op("dve", lambda e: e.tensor_tensor(m1[:], m1[:], m2[:], ALU.add), reads=[m1_tk, m2_tk], writes=[m1_tk])
                S.op("dve", lambda e: e.tensor_tensor(m2[:], sgt[2][:], K.ps[5][:], ALU.mult), reads=[sgt_tk[2], K.pst[5]], writes=[m2_tk])
                S.op("dve", lambda e: e.tensor_tensor(mst[b][:, ts], m1[:], m2[:], ALU.add), reads=[m1_tk, m2_tk], writes=[mst_tk[b]])
            S.dma("sp", mT_d[db * 128:(db + 1) * 128, :], mst[b][:], reads=[mst_tk[b]], writes=[mT_d_tk])
        S.barrier()


def halo_select(K, gathH, gathH_tk, hsel, hsel_tk, haloH, haloH_tk):
    S = K.S
    CW = PW
    with ExitStack() as es:
        sl = [[K.sb(es, f"hs{b}_{i}", [128, CW], BF16) for i in range(3)] for b in range(2)]
        sl_tk = [[Tk() for i in range(3)] for b in range(2)]
        ac = [K.sb(es, f"hacc{b}", [128, CW], BF16) for b in range(2)]
        ac_tk = [Tk(), Tk()]
        for ci in range(NPIECE):
            b = ci % 2
            c0 = ci * CW
            for s_ in range(3):
                S.dma("sp", sl[b][s_][:], gathH[ci, s_ * 128:(s_ + 1) * 128, :], reads=[gathH_tk], writes=[sl_tk[b][s_]])
            S.op("dve", lambda e: e.tensor_scalar(ac[b][:], sl[b][0][:], hsel[:, 0:1], None, ALU.mult),
                 reads=[sl_tk[b][0], hsel_tk], writes=[ac_tk[b]])
            for s_ in (1, 2):
                S.op("dve", lambda e, s_=s_: e.scalar_tensor_tensor(ac[b][:], sl[b][s_][:], hsel[:, s_:s_ + 1], ac[b][:], ALU.mult, ALU.add),
                     reads=[sl_tk[b][s_], hsel_tk, ac_tk[b]], writes=[ac_tk[b]])
            S.dma("sp", haloH[:, c0:c0 + CW], ac[b][:], reads=[ac_tk[b]], writes=[haloH_tk])
        S.barrier()


PARAMS = (("norm_ffn1", [DM]), ("w_ffn1_in", [DM, 2 * DFF]), ("w_ffn1_out", [DFF, DM]), ("norm_mix", [DM]),
          ("w_in", [DM, INW]), ("a_q_norm", [64]), ("a_k_norm", [64]), ("b_q_norm", [64]), ("b_k_norm", [64]),
          ("b_sinks", [8]), ("c_gate_up", [16, 256]), ("c_gate_bias", [256]), ("c_out_norm", [128]),
          ("w_branch", [3, 512, DM]), ("w_out", [DM, DM]), ("norm_ffn2", [DM]), ("w_ffn2_in", [DM, 2 * DFF]),
          ("w_ffn2_out", [DFF, DM]))
INTER = (("qkA", [24, 128, T], BF16), ("vA", [3, 16, 128, 512], BF16), ("qB", [4, 128, T], BF16), ("kB", [128, T], BF16),
         ("vB", [16, 128, 128], BF16), ("qeT", [4, 64, T], BF16), ("keT", [4, 64, T], BF16), ("kd", [64, 32, 256], BF16),
         ("cv", [64, 32, 512], BF16), ("ebl", [64, 128], F32), ("crT", [4, 128, T], BF16))
NLAYER = 2
CC_QOS = "P2"
GROUPS = [[0, 1, 2, 3], [4, 5, 6, 7]]


def build_fused():
    nc = bass.Bass("TRN2", target_bir_lowering=False)
    x_in = nc.dram_tensor("x_in", [DM, T], F32, kind="ExternalInput")
    pos = nc.dram_tensor("pos", [1, T], I32, kind="ExternalInput")
    consts = nc.dram_tensor("consts", [128, NCONST], F32, kind="ExternalInput")
    cf32d = nc.dram_tensor("cf32", [128, 4], F32, kind="ExternalInput")
    m0d = nc.dram_tensor("m0", [128, 256], F32, kind="ExternalInput")
    gmaskd = nc.dram_tensor("gmask", [64, 8], F32, kind="ExternalInput")
    hseld = nc.dram_tensor("hsel", [128, 4], F32, kind="ExternalInput")
    P = {n: nc.dram_tensor(n, [NLAYER] + s, F32, kind="ExternalInput") for n, s in PARAMS}
    x_out = nc.dram_tensor("x_out", [DM, T], F32, kind="ExternalOutput")
    I = {}
    for name, shape, dt in INTER:
        I[name] = nc.dram_tensor(name, shape, dt, kind="Internal")
        I[name + "_tk"] = Tk(name)
    I["expH"] = nc.dram_tensor("expH", [NPIECE, 128, PW], BF16, kind="Internal"); I["expH_tk"] = Tk()
    I["expG"] = nc.dram_tensor("expG", [64, NG], F32, kind="Internal"); I["expG_tk"] = Tk()
    gathH = nc.dram_tensor("gathH", [NPIECE, 4 * 128, PW], BF16, kind="Internal"); gathH_tk = Tk()
    I["allG"] = nc.dram_tensor("gathG", [4 * 64, NG], F32, kind="Internal"); I["allG_tk"] = Tk()
    I["haloH"] = nc.dram_tensor("haloH", [128, NH], BF16, kind="Internal"); I["haloH_tk"] = Tk()
    I["m0"] = m0d
    I["gmask"] = gmaskd
    x1_d = nc.dram_tensor("x1_d", [DM, T], F32, kind="Internal"); x1_tk = Tk()
    xm_d = nc.dram_tensor("xm_d", [DM, T], F32, kind="Internal"); xm_tk = Tk()
    hmT_d = nc.dram_tensor("hmT_d", [DM, T], BF16, kind="Internal"); hmT_tk = Tk()
    mT_d = nc.dram_tensor("mT_d", [DM, T], BF16, kind="Internal"); mT_d_tk = Tk()
    with ExitStack() as es:
        K = KC(nc, es)
        S = K.S
        C = load_consts(K, es, consts)
        cf32 = K.sb(es, "cf32s", [128, 4], F32); cf32_tk = Tk()
        S.dma("sp", cf32[:], cf32d[:, :], writes=[cf32_tk])
        hsel = K.sb(es, "hsel_s", [128, 4], F32); hsel_tk = Tk()
        S.dma("sp", hsel[:], hseld[:, :], writes=[hsel_tk])
        out_tk = Tk()
        for l in range(NLAYER):
            x_src, x_src_tk = (x_in, Tk()) if l == 0 else (xm_d, xm_tk)
            x_dst, x_dst_tk = (xm_d, xm_tk) if l < NLAYER - 1 else (x_out, out_tk)
            with ExitStack() as esl:
                hm = K.sb(esl, "hm", [128, 8, T], BF16); hm_tk = Tk("hm")
                ffn_phase(K, C, l, x_src, x1_d, x_src_tk, x1_tk, P["norm_ffn1"], P["w_ffn1_in"], P["w_ffn1_out"],
                          post=dict(g=P["norm_mix"], hT=hm, hT_tk=hm_tk, h_out=hmT_d, h_out_tk=hmT_tk))
                cosT, cos_tk, sinT, sin_tk = rope_tables(K, esl, pos, cf32, cf32_tk)
                def start_halo_gathers():
                    for pk in range(NPIECE):
                        S.cc(lambda e, pk=pk: e.collective_compute("AllGather", ALU.bypass, replica_groups=GROUPS,
                                                                   ins=[I["expH"][pk]], outs=[gathH[pk]]),
                             reads=[I["expH_tk"]], writes=[gathH_tk])
                inproj_phase(K, C, l, hm, hm_tk, cosT, cos_tk, sinT, sin_tk, P["w_in"], P, I, after_halo=start_halo_gathers)
            S.cc(lambda e: e.collective_compute("AllGather", ALU.bypass, replica_groups=GROUPS,
                                               ins=[I["expG"][:, :]], outs=[I["allG"][:, :]]),
                 reads=[I["expG_tk"]], writes=[I["allG_tk"]])
            with ExitStack() as es2:
                yT = K.sb(es2, "yT", [128, 12, T], BF16); yT_tk = Tk()
                gla_phase(K, C, l, I, P, yT, yT_tk)
                halo_select(K, gathH, gathH_tk, hsel, hsel_tk, I["haloH"], I["haloH_tk"])
                attn_phase(K, C, l, I, P, yT, yT_tk)
                with ExitStack() as es3:
                    hm2 = K.sb(es3, "hm2", [128, 8, T], BF16); hm2_tk = Tk()
                    S.dma("sp", hm2[:], hmT_d[:, :].rearrange("(kc p) t -> p kc t", p=128), reads=[hmT_tk], writes=[hm2_tk])
                    merge_phase(K, C, l, hm2, hm2_tk, yT, yT_tk, P["w_in"], P["w_branch"], mT_d, mT_d_tk)
            ffn_phase(K, C, l, x1_d, x_dst, x1_tk, x_dst_tk, P["norm_ffn2"], P["w_ffn2_in"], P["w_ffn2_out"],
                      pre=dict(w_out=P["w_out"], mT_d=mT_d, mT_d_tk=mT_d_tk))
        S.finish([out_tk], "sp")
        print("fused program: ninst", S.ninst, "nwaits", S.nwaits, "sem gens", S.gen)
    return nc


def host_cf32():
    inv = (10000.0 ** (-np.arange(0, 64, 2, dtype=np.float32) / 64)).astype(np.float32)
    c = np.zeros((128, 4), np.float32)
    for p in range(128):
        c[p, 0] = inv[p % 32]
        c[p, 1] = -1.0 if (p % 64) < 32 else 1.0
    return c


def host_m0(first):
    c = host_consts()
    m = np.zeros((128, 256), np.float32)
    m[:, 0:128] = 0.0 if first else c[:, C_MASKA:C_MASKA + 128]
    m[:, 128:256] = 0.0 if first else c[:, C_MASKB:C_MASKB + 128]
    return m


def host_gmask(j):
    g = np.zeros((64, 8), np.float32)
    for r in range(4):
        m = 1.0 if r < j else 0.0
        g[:, r] = m
        g[:, 4 + r] = 1.0 - m
    return g


def host_hsel(j):
    h = np.zeros((128, 4), np.float32)
    if j > 0:
        h[:, j - 1] = 1.0
    return h


_PROG = {}


def kernel(**inputs):
    NC = 8
    x = np.asarray(inputs["x"], np.float32)
    positions = np.asarray(inputs["positions"], np.int32)
    consts = host_consts()
    cf32 = host_cf32()
    if "fused" not in _PROG:
        _PROG["fused"] = build_fused()
    params = {n: np.ascontiguousarray(np.asarray(inputs[n], np.float32)) for n, _ in PARAMS}
    maps = []
    for c in range(NC):
        b, j = c // 4, c % 4
        m = dict(x_in=np.ascontiguousarray(x[b, j * T:(j + 1) * T].T),
                 pos=np.ascontiguousarray(positions[b:b + 1, j * T:(j + 1) * T]),
                 consts=consts, cf32=cf32, m0=host_m0(j == 0), gmask=host_gmask(j), hsel=host_hsel(j))
        m.update(params)
        maps.append(m)
    res = run_bass_kernel_spmd(_PROG["fused"], maps, core_ids=list(range(NC))).results
    out = np.zeros((2, 4 * T, DM), np.float32)
    for c in range(NC):
        out[c // 4, (c % 4) * T:(c % 4 + 1) * T, :] = np.asarray(res[c]["x_out"]).T
    return out
```

```python
import numpy as np
import concourse.bass as bass
import concourse.mybir as mybir
from concourse.bass_utils import run_bass_kernel_spmd
from contextlib import ExitStack

F32 = mybir.dt.float32
BF16 = mybir.dt.bfloat16
I32 = mybir.dt.int32
AF = mybir.ActivationFunctionType
ALU = mybir.AluOpType

T = 2048
DM = 1024
DFF = 2816
NFB = DFF // 128
INW = 10000
EPS = 1e-6

SEM_LIMIT = 30000
NDS = 12
SAME_ENGINE_SYNC = True
ATT_QK_SIG = "sync"
MASK_ENG = "dve"
DEN_MERGED = False


class Tk:
    __slots__ = ("w", "r", "name", "wsig")

    def __init__(self, name=""):
        self.w = None
        self.r = {}
        self.name = name
        self.wsig = None


class Sched:
    def __init__(self, nc, es):
        self.nc = nc
        self.es = es
        self.eng = {"pe": nc.tensor, "act": nc.scalar, "dve": nc.vector,
                    "pool": nc.gpsimd, "sp": nc.sync}
        self.sem = {}
        self.cnt = {}
        self.gen = {}
        for k in self.eng:
            self.gen[k] = 0
            self._newsem(k)
        self.seen = {k: {} for k in self.eng}
        self.dsem = [es.enter_context(nc.semaphore(f"dsem{i}")) for i in range(NDS)]
        self.dcnt = [0] * NDS
        self.dnext = 0
        self.nwaits = 0
        self.ninst = 0

    def _newsem(self, k):
        self.sem[k] = self.es.enter_context(self.nc.semaphore(f"sem_{k}_{self.gen[k]}"))
        self.cnt[k] = 0
        self.gen[k] += 1

    def _wait(self, e, deps):
        for (sem, key, n) in deps:
            if key == e and not SAME_ENGINE_SYNC:
                continue
            sk = id(sem)
            if self.seen[e].get(sk, 0) >= n:
                continue
            self.eng[e].wait_ge(sem, n)
            self.nwaits += 1
            self.seen[e][sk] = n

    def _deps(self, reads, writes):
        deps = []
        for t in reads:
            if t.w is not None:
                deps.append(t.w)
        for t in writes:
            if t.w is not None:
                deps.append(t.w)
            deps.extend(t.r.values())
        return deps

    def _mark(self, tk, reads, writes):
        for t in writes:
            t.w = tk
            t.r = {}
        for t in reads:
            t.r[id(tk[0])] = tk

    def op(self, e, fn, reads=(), writes=(), sig=(0, 128)):
        deps = self._deps(reads, writes)
        if e == "pe":
            skip = set()
            for t in writes:
                if t.w is not None and t.w[1] == "pe" and t.wsig == sig and sig != "sync":
                    skip.add(t.w)
            deps = [d for d in deps if not (d[1] == "pe" and d in skip)]
        self._wait(e, deps)
        if self.cnt[e] >= SEM_LIMIT:
            self._newsem(e)
        inst = fn(self.eng[e])
        self.cnt[e] += 1
        inst.then_inc(self.sem[e], 1)
        self.ninst += 1
        tk = (self.sem[e], e, self.cnt[e])
        self._mark(tk, reads, writes)
        if e == "pe":
            for t in writes:
                t.wsig = sig
        return tk

    def dma(self, e, out, in_, reads=(), writes=(), **kw):
        i = self.dnext
        self.dnext = (i + 1) % NDS
        deps = self._deps(reads, writes)
        if self.dcnt[i] > 0:
            deps.append((self.dsem[i], "dma", self.dcnt[i]))
        if self.dcnt[i] >= SEM_LIMIT:
            self.dsem[i] = self.es.enter_context(self.nc.semaphore(f"dsem{i}_{self.ninst}"))
            self.dcnt[i] = 0
        self._wait(e, deps)
        inst = self.eng[e].dma_start(out=out, in_=in_, **kw)
        self.dcnt[i] += 16
        inst.then_inc(self.dsem[i], 16)
        self.ninst += 1
        tk = (self.dsem[i], "dma", self.dcnt[i])
        self._mark(tk, reads, writes)
        return tk

    def finish(self, tiles, e="sp"):
        deps = []
        for t in tiles:
            if t.w is not None:
                deps.append(t.w)
            deps.extend(t.r.values())
        self._wait(e, deps)

    def barrier(self):
        deps = [(self.sem[k], k, self.cnt[k]) for k in self.eng if self.cnt[k] > 0]
        deps += [(self.dsem[i], "dma", self.dcnt[i]) for i in range(NDS) if self.dcnt[i] > 0]
        for e in self.eng:
            self._wait(e, [d for d in deps if d[1] != e])


class KC:
    def __init__(self, nc, es):
        self.nc = nc
        self.es = es
        self.S = Sched(nc, es)
        self.ps = [es.enter_context(nc.psum_tensor(f"psb{i}", [128, 512], F32)) for i in range(8)]
        self.pst = [Tk(f"ps{i}") for i in range(8)]
        self.n_uid = 0
        self.dram_tk = {}

    def uid(self, s):
        self.n_uid += 1
        return f"{s}_{self.n_uid}"

    def sb(self, es, name, shape, dt):
        return es.enter_context(self.nc.sbuf_tensor(self.uid(name), shape, dt))

    def dtk(self, name):
        if name not in self.dram_tk:
            self.dram_tk[name] = Tk(name)
        return self.dram_tk[name]


def load_consts(K, es, consts_d):
    nc, S = K.nc, K.S
    c = {}
    c["cb"] = K.sb(es, "cb", [128, consts_d.shape[1]], BF16)
    c["cb_tk"] = Tk("cb")
    S.dma("pool", c["cb"][:], consts_d[:, :], writes=[c["cb_tk"]])
    c["cf"] = K.sb(es, "cf", [128, 8], F32)
    c["cf_tk"] = Tk("cf")
    S.op("dve", lambda e: e.memset(c["cf"][:, 0:1], EPS), writes=[c["cf_tk"]])
    S.op("dve", lambda e: e.memset(c["cf"][:, 1:2], 1.0), writes=[c["cf_tk"]])
    S.op("dve", lambda e: e.memset(c["cf"][:, 2:3], 0.0), writes=[c["cf_tk"]])
    c["ones"] = K.sb(es, "ones", [128, 128], BF16)
    c["ones_tk"] = Tk("ones")
    S.op("dve", lambda e: e.memset(c["ones"][:], 1.0), writes=[c["ones_tk"]])
    return c


C_IDENT = 0
C_PERM = 128
C_BONES = 256
C_MASKA = 384
C_MASKB = 640
C_MASKX = 896
C_TRIL = 1152
NCONST = 1216


def host_consts():
    c = np.zeros((128, NCONST), np.float32)
    c[:, C_IDENT:C_IDENT + 128] = np.eye(128)
    for p in range(128):
        m = p + 32 if (p % 64) < 32 else p - 32
        c[p, C_PERM + m] = 1.0
        c[p, C_BONES + (p // 64) * 64: C_BONES + (p // 64) * 64 + 64] = 1.0
    k = np.arange(128)[:, None]
    q = np.arange(128)[None, :]
    c[:, C_MASKA:C_MASKA + 128] = np.where(k >= q, 1.0, 0.0)
    c[:, C_MASKA + 128:C_MASKA + 256] = np.where(k <= q, 1.0, 0.0)
    c[:, C_MASKB:C_MASKB + 128] = np.where(k > q, 1.0, 0.0)
    c[:, C_MASKB + 128:C_MASKB + 256] = np.where(k <= q, 1.0, 0.0)
    c[:, C_MASKX:C_MASKX + 128] = 0.0
    c[:, C_MASKX + 128:C_MASKX + 256] = np.where(k <= q, 1.0, 0.0)
    c[:64, C_TRIL:C_TRIL + 64] = np.where(k[:64] <= q[:, :64], 1.0, 0.0)
    return c


def rmsnorm_fm(K, C, W, xs, xs_tk, c0, gcol, gcol_tk, out_fn, out_tk, psb):
    S = K.S
    sq, sq_tk, lnv, lnv_tk, rstd, rstd_tk = W["sq"], W["sq_tk"], W["lnv"], W["lnv_tk"], W["rstd"], W["rstd_tk"]
    for kc in range(8):
        S.op("dve", lambda e, kc=kc: e.tensor_tensor(sq[:, kc, :], xs[:, kc, c0:c0 + 512], xs[:, kc, c0:c0 + 512], ALU.mult),
             reads=[xs_tk], writes=[sq_tk])
    for kc in range(8):
        S.op("pe", lambda e, kc=kc: e.matmul(K.ps[psb][:], C["ones"][:], sq[:, kc, :], start=(kc == 0), stop=(kc == 7)),
             reads=[sq_tk, C["ones_tk"]], writes=[K.pst[psb]])
    S.op("act", lambda e: e.activation(lnv[:], K.ps[psb][:], AF.Ln, bias=C["cf"][:, 0:1], scale=1.0 / DM),
         reads=[K.pst[psb], C["cf_tk"]], writes=[lnv_tk])
    S.op("act", lambda e: e.activation(rstd[:], lnv[:], AF.Exp, scale=-0.5), reads=[lnv_tk], writes=[rstd_tk])
    for kc in range(8):
        S.op("dve", lambda e, kc=kc: e.scalar_tensor_tensor(out_fn(kc), xs[:, kc, c0:c0 + 512], gcol[:, kc:kc + 1], rstd[:],
                                                           ALU.mult, ALU.mult),
             reads=[xs_tk, gcol_tk, rstd_tk], writes=[out_tk])


def load_gvec(K, es, name, gd, l):
    g = K.sb(es, name, [128, 8], F32)
    tk = Tk(name)
    src = bass.AP(gd.tensor if hasattr(gd, "tensor") else gd, l * DM, [[1, 128], [128, 8]])
    K.S.dma("sp", g[:], src, writes=[tk], allow_slow_non_contiguous=True)
    return g, tk


def ffn_phase(K, C, l, x_in, x_out, x_tk_in, x_tk_out, g_d, w1_d, w2_d, pre=None, post=None):
    nc, S = K.nc, K.S
    ST = 1024
    with ExitStack() as es:
        xs = K.sb(es, "xs", [128, 8, ST], F32)
        xs_tk = Tk("xs")
        hT = K.sb(es, "hT", [128, 8, ST], BF16)
        hT_tk = [Tk("hT0"), Tk("hT1")]
        aT = K.sb(es, "aT", [128, NFB, ST], BF16)
        aT_tk = [[Tk(f"aT{i}_{t}") for t in range(2)] for i in range(NFB)]
        W = {}
        W["sq"] = K.sb(es, "sq", [128, 8, 512], BF16); W["sq_tk"] = Tk("sq")
        W["lnv"] = K.sb(es, "lnv", [128, 512], F32); W["lnv_tk"] = Tk("lnv")
        W["rstd"] = K.sb(es, "rstd", [128, 512], F32); W["rstd_tk"] = Tk("rstd")
        sg = [K.sb(es, f"sg{i}", [128, 512], F32) for i in range(3)]
        sg_tk = [Tk(f"sg{i}") for i in range(3)]
        w1g = [K.sb(es, f"w1g{i}", [128, 8, 512], BF16) for i in range(2)]
        w1u = [K.sb(es, f"w1u{i}", [128, 8, 512], BF16) for i in range(2)]
        w1_tk = [Tk("w1_0"), Tk("w1_1")]
        w2b = [K.sb(es, f"w2b{i}", [128, NFB, 128], BF16) for i in range(2)]
        w2_tk = [Tk("w2_0"), Tk("w2_1")]
        gcol, gcol_tk = load_gvec(K, es, "gffn", g_d, l)
        if post is not None:
            g2col, g2_tk = load_gvec(K, es, "gpost", post["g"], l)
        if pre is not None:
            mTs = K.sb(es, "mTs", [128, 8, ST], BF16); mTs_tk = Tk()
            wo = [K.sb(es, f"wo{i}", [128, 8, 128], BF16) for i in range(2)]
            wo_tk = [Tk("wo0"), Tk("wo1")]
        w1v = w1_d
        nw1 = 0
        nw2 = 0
        nwo = 0
        pair = 0
        for st in range(T // ST):
            t0 = st * ST
            S.dma("sp", xs[:], x_in[:, t0:t0 + ST].rearrange("(kc p) t -> p kc t", p=128), reads=[x_tk_in], writes=[xs_tk])
            if pre is not None:
                mT, mT_tk = mTs, mTs_tk
                S.dma("sp", mTs[:], pre["mT_d"][:, t0:t0 + ST].rearrange("(kc p) t -> p kc t", p=128), reads=[pre["mT_d_tk"]], writes=[mTs_tk])
                for db in range(8):
                    b = nwo % 2
                    nwo += 1
                    S.dma("pool", wo[b][:], pre["w_out"][l, :, db * 128:(db + 1) * 128].rearrange("(kc p) c -> p kc c", p=128),
                          writes=[wo_tk[b]])
                    for tt in range(2):
                        pb = 6 + tt
                        for kc in range(8):
                            S.op("pe", lambda e, kc=kc, b=b, pb=pb, tt=tt: e.matmul(
                                K.ps[pb][:], wo[b][:, kc, :], mT[:, kc, tt * 512:tt * 512 + 512],
                                start=(kc == 0), stop=(kc == 7)), reads=[wo_tk[b], mT_tk], writes=[K.pst[pb]])
                        S.op("dve", lambda e, db=db, pb=pb, tt=tt: e.tensor_tensor(
                            xs[:, db, tt * 512:tt * 512 + 512], xs[:, db, tt * 512:tt * 512 + 512], K.ps[pb][:], ALU.add),
                            reads=[K.pst[pb], xs_tk], writes=[xs_tk])
            for tt in range(2):
                rmsnorm_fm(K, C, W, xs, xs_tk, tt * 512, gcol, gcol_tk,
                           lambda kc, tt=tt: hT[:, kc, tt * 512:tt * 512 + 512], hT_tk[tt], 6)
            for s0 in range(0, NFB, 4):
                nb = min(4, NFB - s0)
                b = nw1 % 2
                nw1 += 1
                S.dma("pool", w1g[b][:, :, 0:nb * 128],
                      w1v[l, :, s0 * 128:(s0 + nb) * 128].rearrange("(kc p) c -> p kc c", p=128), writes=[w1_tk[b]])
                S.dma("pool", w1u[b][:, :, 0:nb * 128],
                      w1v[l, :, DFF + s0 * 128:DFF + (s0 + nb) * 128].rearrange("(kc p) c -> p kc c", p=128), writes=[w1_tk[b]])
                order = [(i, tt) for i in range(nb) for tt in range(2)] if s0 > 0 else [(i, tt) for tt in range(2) for i in range(nb)]
                for (i, tt) in order:
                    fb = s0 + i
                    if True:
                        pg, pu = 2 * (pair % 3), 2 * (pair % 3) + 1
                        sgi = pair % 3
                        pair += 1
                        for (pb, wt) in ((pg, w1g), (pu, w1u)):
                            for kc in range(8):
                                S.op("pe", lambda e, kc=kc, pb=pb, wt=wt, b=b, i=i, tt=tt: e.matmul(
                                    K.ps[pb][:], wt[b][:, kc, i * 128:(i + 1) * 128], hT[:, kc, tt * 512:tt * 512 + 512],
                                    start=(kc == 0), stop=(kc == 7)), reads=[w1_tk[b], hT_tk[tt]], writes=[K.pst[pb]])
                        S.op("act", lambda e, pg=pg, sgi=sgi: e.activation(sg[sgi][:], K.ps[pg][:], AF.Silu),
                             reads=[K.pst[pg]], writes=[sg_tk[sgi]])
                        S.op("dve", lambda e, pu=pu, sgi=sgi, fb=fb, tt=tt: e.tensor_tensor(
                            aT[:, fb, tt * 512:tt * 512 + 512], sg[sgi][:], K.ps[pu][:], ALU.mult),
                            reads=[sg_tk[sgi], K.pst[pu]], writes=[aT_tk[fb][tt]])
            for db in range(8):
                b = nw2 % 2
                nw2 += 1
                S.dma("pool", w2b[b][:], w2_d[l, :, db * 128:(db + 1) * 128].rearrange("(f p) c -> p f c", p=128),
                      writes=[w2_tk[b]])
                for tt in range(2):
                    pb = 6 + tt
                    for f in range(NFB):
                        S.op("pe", lambda e, f=f, b=b, pb=pb, tt=tt: e.matmul(
                            K.ps[pb][:], w2b[b][:, f, :], aT[:, f, tt * 512:tt * 512 + 512],
                            start=(f == 0), stop=(f == NFB - 1)), reads=[w2_tk[b], aT_tk[f][tt]], writes=[K.pst[pb]])
                    S.op("dve", lambda e, db=db, pb=pb, tt=tt: e.scalar_tensor_tensor(
                        xs[:, db, tt * 512:tt * 512 + 512], K.ps[pb][:], 0.5, xs[:, db, tt * 512:tt * 512 + 512],
                        ALU.mult, ALU.add), reads=[K.pst[pb], xs_tk], writes=[xs_tk])
            S.dma("sp", x_out[:, t0:t0 + ST].rearrange("(kc p) t -> p kc t", p=128), xs[:], reads=[xs_tk], writes=[x_tk_out])
            if post is not None:
                hm, hm_tk = post["hT"], post["hT_tk"]
                for tt in range(2):
                    rmsnorm_fm(K, C, W, xs, xs_tk, tt * 512, g2col, g2_tk,
                               lambda kc, tt=tt: hm[:, kc, t0 + tt * 512:t0 + tt * 512 + 512], hm_tk, 6)
                if post.get("h_out") is not None:
                    S.dma("sp", post["h_out"][:, t0:t0 + ST].rearrange("(kc p) t -> p kc t", p=128), hm[:, :, t0:t0 + ST],
                          reads=[hm_tk], writes=[post["h_out_tk"]])
        S.barrier()


A_DIL = (1, 4, 16)
KH_OFF = (0, 128, 640)
KH_W = 2688
VH_OFF = (0, 1, 5)
NH_KA = 0
NH_VA = 4 * KH_W
NH_KB = NH_VA + 21 * 512
NH_VB = NH_KB + 128
NH = NH_VB + 128
NG = 4 * 129
PW = 2176
NPIECE = NH // PW


def exp_segments(c0, w):
    out = []
    o = 0
    while w > 0:
        k = c0 // PW
        lw = min(w, (k + 1) * PW - c0)
        out.append((k, c0 - k * PW, lw, o))
        c0 += lw
        o += lw
        w -= lw
    return out
CO_AQ, CO_AK, CO_AV = 0, 1536, 3072
CO_BQ, CO_BK, CO_BV = 4608, 5120, 5248
CO_CQ, CO_CK, CO_CV, CO_GL, CO_CR, CO_GATE = 5376, 5632, 5888, 6400, 6416, 6928
MAGIC = 12582912.0
TWO_PI = 6.283185307179586


def rope_tables(K, es, pos_d, cf32, cf32_tk):
    nc, S = K.nc, K.S
    cosT = K.sb(es, "cosT", [128, T], F32)
    sinT = K.sb(es, "sinT", [128, T], F32)
    cos_tk, sin_tk = Tk("cos"), Tk("sin")
    with ExitStack() as es2:
        posi = K.sb(es2, "posi", [128, T], I32)
        ang = K.sb(es2, "ang", [128, T], F32)
        t1 = K.sb(es2, "ropet1", [128, T], F32)
        t2 = K.sb(es2, "ropet2", [128, T], F32)
        p_tk, a_tk, t1_tk, t2_tk = Tk(), Tk(), Tk(), Tk()
        S.dma("sp", posi[:], bass.AP(pos_d, 0, [[0, 128], [1, T]]), writes=[p_tk])
        S.op("dve", lambda e: e.tensor_copy(t1[:], posi[:]), reads=[p_tk], writes=[t1_tk])
        S.op("dve", lambda e: e.tensor_scalar(ang[:], t1[:], cf32[:, 0:1], None, ALU.mult), reads=[t1_tk, cf32_tk], writes=[a_tk])
        for (dst, dst_tk, shift) in ((sinT, sin_tk, 0.0), (cosT, cos_tk, np.pi / 2)):
            S.op("dve", lambda e, shift=shift: e.tensor_scalar(t1[:], ang[:], shift, 1.0 / TWO_PI, ALU.add, ALU.mult),
                 reads=[a_tk], writes=[t1_tk])
            S.op("dve", lambda e: e.tensor_scalar(t1[:], t1[:], MAGIC, -MAGIC, ALU.add, ALU.add), reads=[t1_tk], writes=[t1_tk])
            S.op("dve", lambda e: e.scalar_tensor_tensor(t2[:], t1[:], -TWO_PI, ang[:], ALU.mult, ALU.add),
                 reads=[t1_tk, a_tk], writes=[t2_tk])
            S.op("dve", lambda e, shift=shift: e.tensor_scalar(t2[:], t2[:], shift, 3.14159, ALU.add, ALU.min),
                 reads=[t2_tk], writes=[t2_tk])
            S.op("dve", lambda e: e.tensor_scalar(t2[:], t2[:], -3.14159, None, ALU.max), reads=[t2_tk], writes=[t2_tk])
            S.op("act", lambda e, dst=dst: e.activation(dst[:], t2[:], AF.Sin), reads=[t2_tk], writes=[dst_tk])
        S.op("dve", lambda e: e.tensor_scalar(sinT[:], sinT[:], cf32[:, 1:2], None, ALU.mult), reads=[sin_tk, cf32_tk], writes=[sin_tk])
        S.barrier()
    return cosT, cos_tk, sinT, sin_tk


def inproj_phase(K, C, l, hm, hm_tk, cosT, cos_tk, sinT, sin_tk, w_in, P, O):
    nc, S = K.nc, K.S
    cb, cb_tk = C["cb"], C["cb_tk"]
    with ExitStack() as es:
        NWB = 3
        wbuf = [K.sb(es, f"wst{i}", [128, 8, 512], BF16) for i in range(NWB)]
        wb_tk = [Tk(f"wst{i}") for i in range(NWB)]
        st = {"nw": 0, "pr": 0, "ss": 0, "rot": 0, "cp": 0}

        def load_strip(col0, ncols):
            b = st["nw"] % NWB
            st["nw"] += 1
            S.dma("pool", wbuf[b][:, :, 0:ncols], w_in[l, :, col0:col0 + ncols].rearrange("(kc p) c -> p kc c", p=128),
                  writes=[wb_tk[b]])
            return wbuf[b], wb_tk[b]

        def proj_fm(wb, wtk, coff, M, tt):
            pb = st["pr"] % 4
            st["pr"] += 1
            for kc in range(8):
                S.op("pe", lambda e, kc=kc: e.matmul(K.ps[pb][0:M, :], wb[:, kc, coff:coff + M], hm[:, kc, tt * 512:tt * 512 + 512],
                                                     start=(kc == 0), stop=(kc == 7)), reads=[wtk, hm_tk], writes=[K.pst[pb]])
            return pb

        es1 = ExitStack()
        gh = K.sb(es1, "gh", [128, 4], F32)
        gh_tk = Tk("gh")
        for i, nm in enumerate(("a_q_norm", "a_k_norm", "b_q_norm", "b_k_norm")):
            for half in range(2):
                S.dma("sp", gh[half * 64:half * 64 + 64, i:i + 1], P[nm][l, :].rearrange("(p o) -> p o", o=1), writes=[gh_tk])
        NQ = 4
        sq = [K.sb(es1, f"qsq{i}", [128, 512], BF16) for i in range(NQ)]
        sq_tk = [Tk() for _ in range(NQ)]
        lnv = [K.sb(es1, f"qlnv{i}", [128, 512], F32) for i in range(NQ)]
        lnv_tk = [Tk() for _ in range(NQ)]
        rstd = [K.sb(es1, f"qrstd{i}", [128, 512], F32) for i in range(NQ)]
        rstd_tk = [Tk() for _ in range(NQ)]
        qn = [K.sb(es1, f"qn{i}", [128, 512], BF16) for i in range(NQ)]
        qn_tk = [Tk() for _ in range(NQ)]
        ta = [K.sb(es1, f"qta{i}", [128, 512], F32) for i in range(NQ)]
        ta_tk = [Tk() for _ in range(NQ)]
        tb = [K.sb(es1, f"qtb{i}", [128, 512], F32) for i in range(NQ)]
        tb_tk = [Tk() for _ in range(NQ)]
        stage = [K.sb(es1, f"qstage{i}", [128, T], BF16) for i in range(2)]
        stage_tk = [Tk(), Tk()]
        nqk = [0]

        def qk_block(wb, wtk, coff, gi, d, dst_ap, dst_tk, halo=None):
            sgi = nqk[0] % 2
            nqk[0] += 1
            stg, stg_tk = stage[sgi], stage_tk[sgi]
            TT = range(4)
            pbs = []
            for tt in TT:
                pb = tt
                pbs.append(pb)
                for kc in range(8):
                    S.op("pe", lambda e, kc=kc, pb=pb, tt=tt: e.matmul(K.ps[pb][:], wb[:, kc, coff:coff + 128], hm[:, kc, tt * 512:tt * 512 + 512],
                                                                   start=(kc == 0), stop=(kc == 7)), reads=[wtk, hm_tk], writes=[K.pst[pb]])
            for tt in TT:
                S.op("act", lambda e, tt=tt: e.activation(sq[tt][:], K.ps[pbs[tt]][:], AF.Square), reads=[K.pst[pbs[tt]]], writes=[sq_tk[tt]])
            for tt in TT:
                S.op("pe", lambda e, tt=tt: e.matmul(K.ps[4 + tt][:], cb[:, C_BONES:C_BONES + 128], sq[tt][:], start=True, stop=True),
                     reads=[cb_tk, sq_tk[tt]], writes=[K.pst[4 + tt]])
            for tt in TT:
                S.op("act", lambda e, tt=tt: e.activation(lnv[tt][:], K.ps[4 + tt][:], AF.Ln, bias=C["cf"][:, 0:1], scale=1.0 / 64),
                     reads=[K.pst[4 + tt], C["cf_tk"]], writes=[lnv_tk[tt]])
            for tt in TT:
                S.op("act", lambda e, tt=tt: e.activation(rstd[tt][:], lnv[tt][:], AF.Exp, scale=-0.5), reads=[lnv_tk[tt]], writes=[rstd_tk[tt]])
            for tt in TT:
                S.op("dve", lambda e, tt=tt: e.scalar_tensor_tensor(qn[tt][:], K.ps[pbs[tt]][:], gh[:, gi:gi + 1], rstd[tt][:], ALU.mult, ALU.mult),
                     reads=[K.pst[pbs[tt]], gh_tk, rstd_tk[tt]], writes=[qn_tk[tt]])
            for tt in TT:
                S.op("pe", lambda e, tt=tt: e.matmul(K.ps[4 + tt][:], cb[:, C_PERM:C_PERM + 128], qn[tt][:], start=True, stop=True),
                     reads=[cb_tk, qn_tk[tt]], writes=[K.pst[4 + tt]])
            for tt in TT:
                S.op("dve", lambda e, tt=tt: e.tensor_tensor(ta[tt][:], qn[tt][:], cosT[:, tt * 512:tt * 512 + 512], ALU.mult),
                     reads=[qn_tk[tt], cos_tk], writes=[ta_tk[tt]])
            for tt in TT:
                S.op("dve", lambda e, tt=tt: e.tensor_tensor(tb[tt][:], K.ps[4 + tt][:], sinT[:, tt * 512:tt * 512 + 512], ALU.mult),
                     reads=[K.pst[4 + tt], sin_tk], writes=[tb_tk[tt]])
            n = 512 // d
            for tt in TT:
                outv = stg[:].rearrange("p (r i) -> p r i", r=d)[:, :, tt * n:(tt + 1) * n]
                S.op("dve", lambda e, tt=tt, outv=outv: e.tensor_tensor(outv, ta[tt][:].rearrange("p (i r) -> p r i", r=d),
                                                                        tb[tt][:].rearrange("p (i r) -> p r i", r=d), ALU.add),
                     reads=[ta_tk[tt], tb_tk[tt]], writes=[stg_tk])
            S.dma("sp", dst_ap, stg[:], reads=[stg_tk], writes=[dst_tk])
            if halo is not None:
                nblk = T // (128 * d)
                src = stg[:].rearrange("p (r b i) -> p r b i", r=d, b=nblk)
                for (pk, lc, lw, so) in exp_segments(halo, 128 * d):
                    r0, r1 = so // 128, (so + lw) // 128
                    S.dma("sp", O["expH"][pk, :, lc:lc + lw].rearrange("p (r i) -> p r i", i=128), src[:, r0:r1, nblk - 1, :],
                          reads=[stg_tk], writes=[O["expH_tk"]])

        for which, co, gi, base in (("q", CO_AQ, 0, 0), ("k", CO_AK, 1, 12)):
            for sblk in range(3):
                wb, wtk = load_strip(co + sblk * 512, 512)
                d = A_DIL[sblk]
                for i in range(4):
                    blk = sblk * 4 + i
                    halo = None
                    if which == "k":
                        hc = NH_KA + i * KH_W + KH_OFF[sblk]
                        halo = hc
                    qk_block(wb, wtk, i * 128, gi, d, O["qkA"][base + blk], O["qkA_tk"], halo)
        wb, wtk = load_strip(CO_BQ, 512)
        for i in range(4):
            qk_block(wb, wtk, i * 128, 2, 1, O["qB"][i], O["qB_tk"])
        wb, wtk = load_strip(CO_BK, 256)
        qk_block(wb, wtk, 0, 3, 1, O["kB"][:, :], O["kB_tk"], NH_KB)

        vst = [K.sb(es1, f"vst{i}", [128, 512], BF16) for i in range(3)]
        vst_tk = [Tk(), Tk(), Tk()]
        nv = [0]

        def v_tm(wb, wtk, coff, ncols, tok_ap_fn, M, dst_list):
            pb = st["pr"] % 4
            st["pr"] += 1
            for kc in range(8):
                S.op("pe", lambda e, kc=kc: e.matmul(K.ps[pb][0:M, 0:ncols], tok_ap_fn(kc), wb[:, kc, coff:coff + ncols],
                                                     start=(kc == 0), stop=(kc == 7)), reads=[wtk, hm_tk], writes=[K.pst[pb]])
            b = nv[0] % 3
            nv[0] += 1
            if nv[0] % 2 == 0:
                S.op("act", lambda e: e.activation(vst[b][0:M, 0:ncols], K.ps[pb][0:M, 0:ncols], AF.Copy),
                     reads=[K.pst[pb]], writes=[vst_tk[b]])
            else:
                S.op("dve", lambda e: e.tensor_copy(vst[b][0:M, 0:ncols], K.ps[pb][0:M, 0:ncols]), reads=[K.pst[pb]], writes=[vst_tk[b]])
            for (dap, dtk) in dst_list:
                if isinstance(dap, tuple):
                    for (pk, lc, lw, so) in exp_segments(dap[0], dap[1]):
                        S.dma("sp", O["expH"][pk, :, lc:lc + lw], vst[b][0:M, so:so + lw], reads=[vst_tk[b]], writes=[dtk])
                else:
                    S.dma("sp", dap, vst[b][0:M, 0:ncols], reads=[vst_tk[b]], writes=[dtk])

        for blk in range(16):
            dl = [(O["vB"][blk], O["vB_tk"])]
            if blk == 15:
                dl.append(((NH_VB, 128), O["expH_tk"]))
            v_tm(wb, wtk, 128, 128, lambda kc, blk=blk: hm[:, kc, blk * 128:(blk + 1) * 128], 128, dl)
        for g in range(3):
            d = A_DIL[g]
            nblk = T // (128 * d)
            wb, wtk = load_strip(CO_AV + g * 512, 512)
            for r in range(d):
                for blk in range(nblk):
                    s0 = r + d * 128 * blk
                    dl = [(O["vA"][g, r * nblk + blk], O["vA_tk"])]
                    if blk == nblk - 1:
                        hc = NH_VA + (VH_OFF[g] + r) * 512
                        dl.append(((hc, 512), O["expH_tk"]))
                    v_tm(wb, wtk, 0, 512, lambda kc, s0=s0, d=d: hm[:, kc, s0:s0 + 127 * d + 1:d], 128, dl)

        S.barrier()
        es1.close()
        gla_prep(K, C, l, es, hm, hm_tk, w_in, P, O, load_strip, proj_fm, st)
        S.barrier()


def bc_last(ap, n):
    return bass.AP(ap.tensor, ap.offset, [list(x) for x in ap.ap] + [[0, n]])


def gla_prep(K, C, l, es, hm, hm_tk, w_in, P, O, load_strip, proj_fm, st):
    nc, S = K.nc, K.S
    cb, cb_tk = C["cb"], C["cb_tk"]
    gup = K.sb(es, "gup", [16, 256], BF16); gup_tk = Tk()
    S.dma("pool", gup[:], P["c_gate_up"][l, :, :], writes=[gup_tk])
    cgb = K.sb(es, "cgb", [64, 4], F32); cgb_tk = Tk()
    S.dma("sp", cgb[:], P["c_gate_bias"][l, :].rearrange("(h p) -> p h", p=64), writes=[cgb_tk], allow_slow_non_contiguous=True)
    S.op("dve", lambda e: e.tensor_scalar(cgb[:], cgb[:], -1.0, None, ALU.mult), reads=[cgb_tk], writes=[cgb_tk])
    rmask = K.sb(es, "rmask", [64, 512], F32); rm_tk = Tk()
    S.op("dve", lambda e: e.memset(rmask[:], 1.0), writes=[rm_tk])
    S.op("dve", lambda e: e.memset(rmask[:].rearrange("p (c i) -> p c i", i=64)[:, :, 0:1], 0.0), writes=[rm_tk])
    ebl = K.sb(es, "ebl", [64, 4, 32], F32); ebl_tk = Tk()
    nbl = K.sb(es, "nbl", [64, 4, 32], F32); nbl_tk = Tk()
    ebl2 = K.sb(es, "ebl2", [128, 2, 32], F32); ebl2_tk = Tk()
    kd_all = K.sb(es, "kd_all", [64, 32, 256], BF16); kd_tk = Tk()
    cv_all = K.sb(es, "cv_all", [64, 32, 512], BF16); cv_tk = Tk()
    qes = [K.sb(es, f"qes{p}", [128, T], BF16) for p in range(2)]
    kes = [K.sb(es, f"kes{p}", [128, T], BF16) for p in range(2)]
    qes_tk = [Tk() for _ in range(2)]
    kes_tk = [Tk() for _ in range(2)]
    glow_sb = K.sb(es, "glow_sb", [16, 512], BF16); glow_tk = Tk()
    tmp1 = [K.sb(es, f"gtmp1{p}", [128, 512], F32) for p in range(2)]; tmp1_tk = [Tk(), Tk()]
    tmp2 = [K.sb(es, f"gtmp2{p}", [128, 512], F32) for p in range(2)]; tmp2_tk = [Tk(), Tk()]
    nbuf = [K.sb(es, f"gnbuf{p}", [128, 512], F32) for p in range(2)]; nbuf_tk = [Tk(), Tk()]
    ebuf = [K.sb(es, f"gebuf{p}", [128, 512], F32) for p in range(2)]; ebuf_tk = [Tk(), Tk()]
    enbuf = [K.sb(es, f"genbuf{p}", [128, 512], F32) for p in range(2)]; enbuf_tk = [Tk(), Tk()]
    kdT = [K.sb(es, f"gkdT{p}", [128, 512], BF16) for p in range(2)]; kdT_tk = [Tk(), Tk()]
    cgb2 = K.sb(es, "cgb2", [128, 2], F32); cgb2_tk = Tk()
    S.dma("sp", cgb2[:], P["c_gate_bias"][l, :].rearrange("(q p) -> p q", p=128), writes=[cgb2_tk], allow_slow_non_contiguous=True)
    S.op("dve", lambda e: e.tensor_scalar(cgb2[:], cgb2[:], -1.0, None, ALU.mult), reads=[cgb2_tk], writes=[cgb2_tk])
    rmask2 = K.sb(es, "rmask2", [128, 512], F32); rm2_tk = Tk()
    S.op("dve", lambda e: e.memset(rmask2[:], 1.0), writes=[rm2_tk])
    S.op("dve", lambda e: e.memset(rmask2[:].rearrange("p (c i) -> p c i", i=64)[:, :, 0:1], 0.0), writes=[rm2_tk])
    psT = [K.ps[6][:].bitcast(BF16), K.ps[7][:].bitcast(BF16)]

    wq, wq_tk = load_strip(CO_CQ, 512)
    wgl, wgl_tk = load_strip(CO_GL, 16)
    PR = range(2)
    for tt in range(4):
        ts = slice(tt * 512, tt * 512 + 512)
        pbg = proj_fm(wgl, wgl_tk, 0, 16, tt)
        S.op("act", lambda e: e.activation(glow_sb[:], K.ps[pbg][0:16, :], AF.Copy), reads=[K.pst[pbg]], writes=[glow_tk])
        pbs = []
        for p in PR:
            pb = st["pr"] % 4
            st["pr"] += 1
            pbs.append(pb)
            S.op("pe", lambda e, p=p, pb=pb: e.matmul(K.ps[pb][:], gup[:, 128 * p:128 * p + 128], glow_sb[:], start=True, stop=True),
                 reads=[gup_tk, glow_tk], writes=[K.pst[pb]], sig="sync")
        for p in PR:
            S.op("act", lambda e, p=p: e.activation(tmp1[p][:], K.ps[pbs[p]][:], AF.Exp, bias=cgb2[:, p:p + 1], scale=-1.0),
                 reads=[K.pst[pbs[p]], cgb2_tk], writes=[tmp1_tk[p]])
        for p in PR:
            S.op("act", lambda e, p=p: e.activation(tmp2[p][:], tmp1[p][:], AF.Ln, bias=C["cf"][:, 1:2], scale=1.0),
                 reads=[tmp1_tk[p], C["cf_tk"]], writes=[tmp2_tk[p]])
        for p in PR:
            S.op("dve", lambda e, p=p: e.tensor_tensor_scan(nbuf[p][:], rmask2[:], tmp2[p][:], 0.0, ALU.mult, ALU.add),
                 reads=[rm2_tk, tmp2_tk[p]], writes=[nbuf_tk[p]])
        for p in PR:
            S.op("act", lambda e, p=p: e.activation(ebuf[p][:], nbuf[p][:], AF.Exp, scale=-1.0 / 16), reads=[nbuf_tk[p]], writes=[ebuf_tk[p]])
        for p in PR:
            S.op("act", lambda e, p=p: e.activation(enbuf[p][:], nbuf[p][:], AF.Exp, scale=1.0 / 16), reads=[nbuf_tk[p]], writes=[enbuf_tk[p]])
        pqs = [proj_fm(wq, wq_tk, 128 * p, 128, tt) for p in PR]
        for p in PR:
            S.op("dve", lambda e, p=p: e.scalar_tensor_tensor(qes[p][:, ts], K.ps[pqs[p]][:], 0.125, ebuf[p][:], ALU.mult, ALU.mult),
                 reads=[K.pst[pqs[p]], ebuf_tk[p]], writes=[qes_tk[p]])
        pks = [proj_fm(wq, wq_tk, 256 + 128 * p, 128, tt) for p in PR]
        for p in PR:
            S.op("dve", lambda e, p=p: e.tensor_tensor(kes[p][:, ts], K.ps[pks[p]][:], enbuf[p][:], ALU.mult),
                 reads=[K.pst[pks[p]], enbuf_tk[p]], writes=[kes_tk[p]])
        for p in PR:
            S.op("dve", lambda e, p=p: e.tensor_copy(ebl2[:, p, 8 * tt:8 * tt + 8], ebuf[p][:, 63::64]), reads=[ebuf_tk[p]], writes=[ebl2_tk])
            for hh in range(2):
                S.op("dve", lambda e, p=p, hh=hh: e.tensor_copy(ebl[:, 2 * p + hh, 8 * tt:8 * tt + 8], ebuf[p][64 * hh:64 * hh + 64, 63::64]),
                     reads=[ebuf_tk[p]], writes=[ebl_tk])
                S.op("dve", lambda e, p=p, hh=hh: e.tensor_copy(nbl[:, 2 * p + hh, 8 * tt:8 * tt + 8], nbuf[p][64 * hh:64 * hh + 64, 63::64]),
                     reads=[nbuf_tk[p]], writes=[nbl_tk])
        for p in PR:
            S.op("dve", lambda e, p=p: e.tensor_tensor(kdT[p][:].rearrange("p (c i) -> p c i", i=64),
                                                       kes[p][:, ts].rearrange("p (c i) -> p c i", i=64),
                                                       bc_last(ebl2[:, p, 8 * tt:8 * tt + 8], 64), ALU.mult),
                 reads=[kes_tk[p], ebl2_tk], writes=[kdT_tk[p]])
        for p in PR:
            for c in range(8):
                S.op("pe", lambda e, c=c, p=p: e.transpose(psT[p][0:64, c * 128:(c + 1) * 128], kdT[p][:, c * 64:(c + 1) * 64], cb[:, C_IDENT:C_IDENT + 128]),
                     reads=[kdT_tk[p], cb_tk], writes=[K.pst[6 + p]])
            S.op("act", lambda e, p=p: e.activation(kd_all[:, 8 * tt:8 * tt + 8, 128 * p:128 * p + 128],
                                                    psT[p][0:64, 0:1024].rearrange("q (c i) -> q c i", i=128), AF.Copy),
                 reads=[K.pst[6 + p]], writes=[kd_tk])
    for h in range(4):
        p, hh = h // 2, h % 2
        S.dma("sp", O["qeT"][h], qes[p][64 * hh:64 * hh + 64, :], reads=[qes_tk[p]], writes=[O["qeT_tk"]])
        S.dma("sp", O["keT"][h], kes[p][64 * hh:64 * hh + 64, :], reads=[kes_tk[p]], writes=[O["keT_tk"]])
    S.dma("sp", O["kd"][:, :, :], kd_all[:], reads=[kd_tk], writes=[O["kd_tk"]])
    S.dma("sp", O["ebl"][:, :], ebl[:].rearrange("p h c -> p (h c)"), reads=[ebl_tk], writes=[O["ebl_tk"]])

    wv, wv_tk = load_strip(CO_CV, 512)
    for c in range(32):
        pb = st["pr"] % 4
        st["pr"] += 1
        for kc in range(8):
            S.op("pe", lambda e, kc=kc: e.matmul(K.ps[pb][0:64, :], hm[:, kc, c * 64:(c + 1) * 64], wv[:, kc, :],
                                                 start=(kc == 0), stop=(kc == 7)), reads=[wv_tk, hm_tk], writes=[K.pst[pb]])
        if c % 2 == 0:
            S.op("act", lambda e: e.activation(cv_all[:, c, :], K.ps[pb][0:64, :], AF.Copy), reads=[K.pst[pb]], writes=[cv_tk])
        else:
            S.op("dve", lambda e: e.tensor_copy(cv_all[:, c, :], K.ps[pb][0:64, :]), reads=[K.pst[pb]], writes=[cv_tk])
    S.dma("sp", O["cv"][:, :, :], cv_all[:], reads=[cv_tk], writes=[O["cv_tk"]])

    wr, wr_tk = load_strip(CO_CR, 512)
    crs = [K.sb(es, f"crs{i}", [128, T], BF16) for i in range(2)]
    crs_tk = [Tk(), Tk()]
    for h in range(4):
        for tt in range(4):
            pb = proj_fm(wr, wr_tk, 128 * h, 128, tt)
            S.op("act", lambda e: e.activation(crs[h % 2][:, tt * 512:tt * 512 + 512], K.ps[pb][:], AF.Silu),
                 reads=[K.pst[pb]], writes=[crs_tk[h % 2]])
        S.dma("sp", O["crT"][h], crs[h % 2][:], reads=[crs_tk[h % 2]], writes=[O["crT_tk"]])

    Sst = K.sb(es, "Sst", [64, 4, 128], F32); Sst_tk = Tk()
    S.op("dve", lambda e: e.memset(Sst[:], 0.0), writes=[Sst_tk])
    for c in range(32):
        pb = st["pr"] % 4
        st["pr"] += 1
        for h in range(4):
            S.op("pe", lambda e, h=h: e.matmul(K.ps[pb][0:64, 128 * h:128 * h + 128], kd_all[:, c, 64 * h:64 * h + 64],
                                               cv_all[:, c, 128 * h:128 * h + 128], start=True, stop=True),
                 reads=[kd_tk, cv_tk], writes=[K.pst[pb]], sig="sync")
        S.op("dve", lambda e: e.tensor_tensor(Sst[:], Sst[:], bc_last(ebl[:, :, c], 128), ALU.mult), reads=[ebl_tk, Sst_tk], writes=[Sst_tk])
        S.op("dve", lambda e: e.tensor_tensor(Sst[:], Sst[:], K.ps[pb][0:64, :].rearrange("p (h e) -> p h e", h=4), ALU.add),
             reads=[K.pst[pb], Sst_tk], writes=[Sst_tk])
    gst = K.sb(es, "gst", [64, 4, 129], F32); gst_tk = Tk()
    dsum = K.sb(es, "dsum", [64, 4], F32); dsum_tk = Tk()
    S.op("dve", lambda e: e.tensor_reduce(dsum[:], nbl[:], mybir.AxisListType.X, ALU.add), reads=[nbl_tk], writes=[dsum_tk])
    S.op("act", lambda e: e.activation(gst[:, :, 128], dsum[:], AF.Exp, scale=-1.0 / 16), reads=[dsum_tk], writes=[gst_tk])
    S.op("dve", lambda e: e.tensor_copy(gst[:, :, 0:128], Sst[:]), reads=[Sst_tk], writes=[gst_tk])
    S.dma("sp", O["expG"][:, :], gst[:].rearrange("p h e -> p (h e)"), reads=[gst_tk], writes=[O["expG_tk"]])


def attn_phase(K, C, l, I, P, yT, yT_tk):
    nc, S = K.nc, K.S
    cb, cb_tk = C["cb"], C["cb_tk"]
    with ExitStack() as es:
        qbuf = [[K.sb(es, f"qbuf{i}_{hh}", [128, T], BF16) for hh in range(2)] for i in range(2)]
        kbuf = [K.sb(es, f"kbuf{i}", [128, 2 * T], BF16) for i in range(2)]
        vbuf = [K.sb(es, f"vbuf{i}", [128, 32, 128], BF16) for i in range(2)]
        q_tk = [Tk(), Tk()]; k_tk = [Tk(), Tk()]; v_tk = [Tk(), Tk()]
        for i in range(2):
            for hh in range(2):
                z0 = 64 * (1 - hh)
                S.op("dve", lambda e, i=i, hh=hh, z0=z0: e.memset(qbuf[i][hh][z0:z0 + 64, :], 0.0), writes=[q_tk[i]])
        acc = K.sb(es, "acc", [64, 4, T], F32); acc_tk = Tk()
        pT = [K.sb(es, f"pT{i}", [128, 512], BF16) for i in range(2)]
        pT_tk = [Tk(), Tk()]
        m0 = K.sb(es, "m0", [128, 256], BF16); m0_tk = Tk()
        S.dma("pool", m0[:], I["m0"][:, :], writes=[m0_tk])
        esink = K.sb(es, "esink", [64, 8], F32); es_tk = Tk()
        S.dma("sp", esink[:], bass.AP(P["b_sinks"].tensor if hasattr(P["b_sinks"], "tensor") else P["b_sinks"], l * 8, [[0, 64], [1, 8]]),
              writes=[es_tk])
        S.op("act", lambda e: e.activation(esink[:], esink[:], AF.Exp), reads=[es_tk], writes=[es_tk])
        lden = K.sb(es, "lden", [64, T], F32); lden_tk = Tk()
        mk4 = K.sb(es, "mk4", [128, 4, 512], BF16); mk4_tk = Tk()
        for vi, (mcol, m0o) in enumerate(((C_MASKA, 0), (C_MASKB, 128))):
            for hh in range(2):
                S.op("dve", lambda e, vi=vi, hh=hh, mcol=mcol: e.tensor_copy(mk4[:, 2 * vi, hh * 256:hh * 256 + 256], cb[:, mcol:mcol + 256]),
                     reads=[cb_tk], writes=[mk4_tk])
                S.op("dve", lambda e, vi=vi, hh=hh, mcol=mcol: e.tensor_copy(mk4[:, 2 * vi + 1, hh * 256 + 128:hh * 256 + 256], cb[:, mcol + 128:mcol + 256]),
                     reads=[cb_tk], writes=[mk4_tk])
                S.op("dve", lambda e, vi=vi, hh=hh, m0o=m0o: e.tensor_copy(mk4[:, 2 * vi + 1, hh * 256:hh * 256 + 128], m0[:, m0o:m0o + 128]),
                     reads=[m0_tk], writes=[mk4_tk])
        nu = [0]
        npq = [0]

        def unit(branch, g, hp, first_group):
            d = A_DIL[g] if branch == "A" else 1
            nblk = T // (128 * d)
            W = (nblk + 1) * 128
            st = {}

            def setup():
                bi = nu[0] % 2
                nu[0] += 1
                qb, kb, vb = qbuf[bi], kbuf[bi], vbuf[bi]
                kv = kb[:, 0:d * W].rearrange("p (r w) -> p r w", r=d)
                qv = [qb[hh][:].rearrange("p (r w) -> p r w", r=d) for hh in range(2)]
                vv = vb[:, 0:d * (nblk + 1), :].rearrange("p (r b) c -> p r b c", r=d)
                if branch == "A":
                    blk = g * 4 + hp
                    for hh in range(2):
                        S.dma("sp", qb[hh][64 * hh:64 * hh + 64, :], I["qkA"][blk, 64 * hh:64 * hh + 64, :], reads=[I["qkA_tk"]], writes=[q_tk[bi]])
                    S.dma("sp", kv[:, :, 128:], I["qkA"][12 + blk].rearrange("p (r w) -> p r w", r=d), reads=[I["qkA_tk"]], writes=[k_tk[bi]])
                    hc = NH_KA + hp * KH_W + KH_OFF[g]
                    S.dma("sp", kv[:, :, 0:128], I["haloH"][:, hc:hc + 128 * d].rearrange("p (r i) -> p r i", r=d),
                          reads=[I["haloH_tk"]], writes=[k_tk[bi]])
                    for r in range(d):
                        S.dma("sp", vv[:, r, 1:, :], I["vA"][g, r * nblk:(r + 1) * nblk, :, hp * 128:(hp + 1) * 128].rearrange("b p c -> p b c"),
                              reads=[I["vA_tk"]], writes=[v_tk[bi]])
                    hv = NH_VA + VH_OFF[g] * 512
                    S.dma("sp", vv[:, :, 0, :], I["haloH"][:, hv:hv + d * 512].rearrange("p (r c) -> p r c", r=d)[:, :, hp * 128:(hp + 1) * 128],
                          reads=[I["haloH_tk"]], writes=[v_tk[bi]])
                    st["vcol"] = lambda hh: slice(hh * 64, hh * 64 + 64)
                else:
                    kvh = hp // 2
                    for hh in range(2):
                        S.dma("sp", qb[hh][64 * hh:64 * hh + 64, :], I["qB"][hp, 64 * hh:64 * hh + 64, :], reads=[I["qB_tk"]], writes=[q_tk[bi]])
                    for half in range(2):
                        S.dma("sp", kv[half * 64:half * 64 + 64, :, 128:], I["kB"][kvh * 64:kvh * 64 + 64, :].rearrange("p (r w) -> p r w", r=1),
                              reads=[I["kB_tk"]], writes=[k_tk[bi]])
                        S.dma("sp", kv[half * 64:half * 64 + 64, :, 0:128],
                              I["haloH"][kvh * 64:kvh * 64 + 64, NH_KB:NH_KB + 128].rearrange("p (r i) -> p r i", r=1),
                              reads=[I["haloH_tk"]], writes=[k_tk[bi]])
                    S.dma("sp", vv[:, 0, 1:, :], I["vB"][:, :, :].rearrange("b p c -> p b c"), reads=[I["vB_tk"]], writes=[v_tk[bi]])
                    S.dma("sp", vv[:, 0, 0, :], I["haloH"][:, NH_VB:NH_VB + 128], reads=[I["haloH_tk"]], writes=[v_tk[bi]])
                    st["vcol"] = lambda hh: slice(kvh * 64, kvh * 64 + 64)
                st.update(bi=bi, kv=kv, qv=qv, vv=vv)

            items = []
            for r in range(d):
                for b in range(nblk):
                    it = {}

                    def front(r=r, b=b, it=it):
                        if not st:
                            setup()
                        bi, kv, qv = st["bi"], st["kv"], st["qv"]
                        i2 = npq[0] % 2
                        npq[0] += 1
                        it["i2"] = i2
                        pss = 2 * i2
                        for hh in range(2):
                            bp = 64 * hh
                            for half in range(2):
                                o_ap = K.ps[pss][:, (hh * 2 + half) * 128:(hh * 2 + half + 1) * 128]
                                S.op("pe", lambda e, o_ap=o_ap, hh=hh, half=half: e.matmul(
                                    o_ap, kv[:, r, (b + half) * 128:(b + half + 1) * 128], qv[hh][:, r, b * 128:(b + 1) * 128],
                                    start=True, stop=True), reads=[k_tk[bi], q_tk[bi]], writes=[K.pst[pss]])
                        S.op("act", lambda e: e.activation(pT[i2][:], K.ps[pss][:], AF.Exp, scale=0.125), reads=[K.pst[pss]], writes=[pT_tk[i2]])
                        mv = (0 if branch == "A" else 2) + (1 if b == 0 else 0)
                        S.op(MASK_ENG, lambda e, mv=mv: e.tensor_tensor(pT[i2][:], pT[i2][:], mk4[:, mv, :], ALU.mult),
                             reads=[pT_tk[i2], mk4_tk], writes=[pT_tk[i2]])

                    def back(r=r, b=b, it=it):
                        bi, vv, vcol = st["bi"], st["vv"], st["vcol"]
                        i2 = it["i2"]
                        pso = 2 * i2 + 1
                        for hh in range(2):
                            o_ap = K.ps[pso][0:64, hh * 128:(hh + 1) * 128]
                            for half in range(2):
                                S.op("pe", lambda e, o_ap=o_ap, hh=hh, half=half: e.matmul(
                                    o_ap, vv[:, r, b + half, vcol(hh)], pT[i2][:, (hh * 2 + half) * 128:(hh * 2 + half + 1) * 128],
                                    start=(half == 0), stop=(half == 1)), reads=[v_tk[bi], pT_tk[i2]], writes=[K.pst[pso]])
                        for half in range(2):
                            S.op("pe", lambda e, half=half: e.matmul(
                                K.ps[pso][0:64, 256:512].rearrange("p (h q) -> p h q", h=2), C["ones"][:, 0:64],
                                pT[i2][:].rearrange("p (h f q) -> p h f q", h=2, f=2)[:, :, half, :],
                                start=(half == 0), stop=(half == 1)), reads=[pT_tk[i2], C["ones_tk"]], writes=[K.pst[pso]])
                        av = acc[:].rearrange("p n (i r) -> p n r i", r=d)[:, :, r, b * 128:(b + 1) * 128]
                        pv = K.ps[pso][0:64, :].rearrange("p (n i) -> p n i", n=4)
                        if first_group:
                            S.op("act", lambda e: e.activation(av, pv, AF.Copy), reads=[K.pst[pso]], writes=[acc_tk])
                        else:
                            S.op("dve", lambda e: e.tensor_tensor(av, av, pv, ALU.add), reads=[K.pst[pso], acc_tk], writes=[acc_tk])

                    items.append([front, back, None])
            return items

        def finalize(branch, hp):
            for hh in range(2):
                den = acc[:, 2 + hh, :]
                if branch == "A":
                    S.op("act", lambda e: e.activation(lden[:], den, AF.Ln), reads=[acc_tk], writes=[lden_tk])
                else:
                    hq = 2 * hp + hh
                    S.op("act", lambda e: e.activation(lden[:], den, AF.Ln, bias=esink[:, hq:hq + 1], scale=1.0),
                         reads=[acc_tk, es_tk], writes=[lden_tk])
                S.op("act", lambda e: e.activation(lden[:], lden[:], AF.Exp, scale=-1.0), reads=[lden_tk], writes=[lden_tk])
                yb = hp if branch == "A" else 4 + hp
                S.op("dve", lambda e: e.tensor_tensor(yT[64 * hh:64 * hh + 64, yb, :], acc[:, hh, :], lden[:], ALU.mult),
                     reads=[acc_tk, lden_tk], writes=[yT_tk])

        items = []
        for hp in range(4):
            for g in range(3):
                items += unit("A", g, hp, g == 0)
            items[-1][2] = ("A", hp)
        for hp in range(4):
            items += unit("B", 0, hp, True)
            items[-1][2] = ("B", hp)
        prev = None
        for it in items:
            it[0]()
            if prev is not None:
                prev[1]()
                if prev[2] is not None:
                    finalize(*prev[2])
            prev = it
        prev[1]()
        finalize(*prev[2])
        S.barrier()


def gla_phase(K, C, l, I, P, yT, yT_tk):
    nc, S = K.nc, K.S
    cb, cb_tk = C["cb"], C["cb_tk"]
    with ExitStack() as es:
        qe = K.sb(es, "gqe", [128, 4, T], BF16); qe_tk = Tk()
        ke = K.sb(es, "gke", [128, 4, T], BF16); ke_tk = Tk()
        cv = K.sb(es, "gcv", [128, 32, 512], BF16); cv_tk = Tk()
        kd = K.sb(es, "gkd", [128, 32, 256], BF16); kd_tk = Tk()
        zc = C["cf"][64:128, 2:3]
        def zfill(eng, t, tk, n):
            v = t[64:128, :, :].rearrange("p a b -> p (a b)") if len(t.shape) == 3 else t[64:128, :]
            if eng == "act":
                S.op("act", lambda e: e.activation(v, bass.AP(zc.tensor, zc.offset, [list(zc.ap[0]), [0, n]]), AF.Copy), reads=[C["cf_tk"]], writes=[tk])
            else:
                S.op("dve", lambda e: e.memset(v, 0.0), writes=[tk])
        zfill("dve", qe, qe_tk, 4 * T)
        zfill("dve", ke, ke_tk, 4 * T)
        zfill("act", cv, cv_tk, 32 * 512)
        zfill("dve", kd, kd_tk, 32 * 256)
        ebl = K.sb(es, "gebl", [64, 4, 32], F32); ebl_tk = Tk()
        gall = K.sb(es, "gall", [64, 4, NG], F32); gall_tk = Tk()
        gm = K.sb(es, "gm", [64, 8], F32); gm_tk = Tk()
        Sall = K.sb(es, "Sall", [128, 32, 512], BF16); Sall_tk = Tk()
        zfill("act", Sall, Sall_tk, 32 * 512)
        Sst = K.sb(es, "Sst2", [64, 4, 128], F32); Sst_tk = Tk()
        acf = K.sb(es, "acf", [64, 4], F32); acf_tk = Tk()
        gco = K.sb(es, "gco", [128, 1], F32); gco_tk = Tk()
        for h in range(4):
            S.dma("sp", qe[0:64, h, :], I["qeT"][h], reads=[I["qeT_tk"]], writes=[qe_tk])
            S.dma("sp", ke[0:64, h, :], I["keT"][h], reads=[I["keT_tk"]], writes=[ke_tk])
        S.dma("sp", cv[0:64], I["cv"][:, :, :], reads=[I["cv_tk"]], writes=[cv_tk])
        S.dma("sp", kd[0:64], I["kd"][:, :, :], reads=[I["kd_tk"]], writes=[kd_tk])
        S.dma("sp", ebl[:].rearrange("p h c -> p (h c)"), I["ebl"][:, :], reads=[I["ebl_tk"]], writes=[ebl_tk])
        S.dma("sp", gall[:], I["allG"][:, :].rearrange("(r p) c -> p r c", p=64), reads=[I["allG_tk"]], writes=[gall_tk])
        S.dma("sp", gm[:], I["gmask"][:, :], writes=[gm_tk])
        S.dma("sp", gco[:], P["c_out_norm"][l, :].rearrange("(p o) -> p o", o=1), writes=[gco_tk])
        S.op("dve", lambda e: e.memset(Sst[:], 0.0), writes=[Sst_tk])
        for r in range(4):
            gv = gall[:, r, :].rearrange("p (h e) -> p h e", h=4)
            S.op("dve", lambda e, gv=gv, r=r: e.tensor_scalar(acf[:], gv[:, :, 128], gm[:, r:r + 1], gm[:, 4 + r:5 + r], ALU.mult, ALU.add),
                 reads=[gall_tk, gm_tk], writes=[acf_tk])
            S.op("dve", lambda e: e.tensor_tensor(Sst[:], Sst[:], bc_last(acf[:, :], 128), ALU.mult), reads=[acf_tk, Sst_tk], writes=[Sst_tk])
            S.op("dve", lambda e, gv=gv, r=r: e.scalar_tensor_tensor(Sst[:], gv[:, :, 0:128], gm[:, r:r + 1], Sst[:], ALU.mult, ALU.add),
                 reads=[gall_tk, gm_tk, Sst_tk], writes=[Sst_tk])
        for c in range(32):
            pb = 6 + (c % 2)
            S.op("act", lambda e: e.activation(Sall[0:64, c, :], Sst[:].rearrange("p h e -> p (h e)"), AF.Copy), reads=[Sst_tk], writes=[Sall_tk])
            for h in range(4):
                S.op("pe", lambda e, h=h: e.matmul(K.ps[pb][0:64, 128 * h:128 * h + 128], kd[:, c, 64 * h:64 * h + 64],
                                                   cv[:, c, 128 * h:128 * h + 128], start=True, stop=True),
                     reads=[kd_tk, cv_tk], writes=[K.pst[pb]])
            S.op("dve", lambda e: e.tensor_tensor(Sst[:], Sst[:], bc_last(ebl[:, :, c], 128), ALU.mult), reads=[ebl_tk, Sst_tk], writes=[Sst_tk])
            S.op("dve", lambda e: e.tensor_tensor(Sst[:], Sst[:], K.ps[pb][0:64, :].rearrange("p (h e) -> p h e", h=4), ALU.add),
                 reads=[K.pst[pb], Sst_tk], writes=[Sst_tk])
        crb = [K.sb(es, f"crb{i}", [128, T], BF16) for i in range(2)]
        crb_tk = [Tk(), Tk()]
        attm = [K.sb(es, f"attm{i}", [128, 512], BF16) for i in range(2)]
        attm_tk = [Tk(), Tk()]
        for i in range(2):
            zfill("dve", attm[i], attm_tk[i], 512)
        osq = [K.sb(es, f"osq{i}", [128, 512], BF16) for i in range(2)]
        osq_tk = [Tk(), Tk()]
        oln = [K.sb(es, f"oln{i}", [128, 512], F32) for i in range(2)]
        oln_tk = [Tk(), Tk()]
        ors = [K.sb(es, f"ors{i}", [128, 512], F32) for i in range(2)]
        ors_tk = [Tk(), Tk()]
        oy = [K.sb(es, f"oy{i}", [128, 512], F32) for i in range(2)]
        oy_tk = [Tk(), Tk()]
        tril = cb[0:64, C_TRIL:C_TRIL + 64]
        tril_b = bass.AP(tril.tensor, tril.offset, [list(tril.ap[0]), [0, 8], list(tril.ap[1])])
        n = 0
        for h in range(4):
            S.dma("sp", crb[h % 2][:], I["crT"][h], reads=[I["crT_tk"]], writes=[crb_tk[h % 2]])
            for tt in range(4):
                i2 = n % 2
                n += 1
                pa, po, pss = i2, 2 + i2, 4 + i2
                for c8 in range(8):
                    c = tt * 8 + c8
                    cs = slice(c * 64, c * 64 + 64)
                    S.op("pe", lambda e, c8=c8, cs=cs: e.matmul(K.ps[pa][0:64, c8 * 64:c8 * 64 + 64], ke[:, h, cs], qe[:, h, cs], start=True, stop=True),
                         reads=[ke_tk, qe_tk], writes=[K.pst[pa]])
                S.op("dve", lambda e: e.tensor_tensor(attm[i2][0:64, :].rearrange("p (c i) -> p c i", i=64),
                                                      K.ps[pa][0:64, :].rearrange("p (c i) -> p c i", i=64), tril_b, ALU.mult),
                     reads=[K.pst[pa], cb_tk], writes=[attm_tk[i2]])
                for c8 in range(8):
                    c = tt * 8 + c8
                    cs = slice(c * 64, c * 64 + 64)
                    o_ap = K.ps[po][:, c8 * 64:c8 * 64 + 64]
                    S.op("pe", lambda e, o_ap=o_ap, c=c, c8=c8: e.matmul(o_ap, cv[:, c, 128 * h:128 * h + 128], attm[i2][:, c8 * 64:c8 * 64 + 64],
                                                                         start=True, stop=False), reads=[cv_tk, attm_tk[i2]], writes=[K.pst[po]])
                    S.op("pe", lambda e, o_ap=o_ap, c=c, cs=cs: e.matmul(o_ap, Sall[:, c, 128 * h:128 * h + 128], qe[:, h, cs], start=False, stop=True),
                         reads=[Sall_tk, qe_tk], writes=[K.pst[po]])
                S.op("act", lambda e: e.activation(osq[i2][:], K.ps[po][:], AF.Square), reads=[K.pst[po]], writes=[osq_tk[i2]])
                S.op("pe", lambda e: e.matmul(K.ps[pss][:], C["ones"][:], osq[i2][:], start=True, stop=True),
                     reads=[C["ones_tk"], osq_tk[i2]], writes=[K.pst[pss]])
                S.op("act", lambda e: e.activation(oln[i2][:], K.ps[pss][:], AF.Ln, bias=C["cf"][:, 0:1], scale=1.0 / 128),
                     reads=[K.pst[pss], C["cf_tk"]], writes=[oln_tk[i2]])
                S.op("act", lambda e: e.activation(ors[i2][:], oln[i2][:], AF.Exp, scale=-0.5), reads=[oln_tk[i2]], writes=[ors_tk[i2]])
                S.op("dve", lambda e: e.scalar_tensor_tensor(oy[i2][:], K.ps[po][:], gco[:, 0:1], ors[i2][:], ALU.mult, ALU.mult),
                     reads=[K.pst[po], gco_tk, ors_tk[i2]], writes=[oy_tk[i2]])
                S.op("dve", lambda e: e.tensor_tensor(yT[:, 8 + h, tt * 512:tt * 512 + 512], oy[i2][:], crb[h % 2][:, tt * 512:tt * 512 + 512], ALU.mult),
                     reads=[oy_tk[i2], crb_tk[h % 2]], writes=[yT_tk])
        S.barrier()


def merge_phase(K, C, l, hm, hm_tk, yT, yT_tk, w_in, w_branch, mT_d, mT_d_tk):
    nc, S = K.nc, K.S
    with ExitStack() as es:
        wg = [[K.sb(es, f"wg{b}_{i}", [128, 8, 128], BF16) for i in range(3)] for b in range(2)]
        wbr = [[K.sb(es, f"wbr{b}_{i}", [128, 4, 128], BF16) for i in range(3)] for b in range(2)]
        w_tk = [Tk(), Tk()]
        sgt = [K.sb(es, f"sgt{i}", [128, 512], F32) for i in range(3)]
        sgt_tk = [Tk() for _ in range(3)]
        m1 = K.sb(es, "mm1", [128, 512], F32); m1_tk = Tk()
        m2 = K.sb(es, "mm2", [128, 512], F32); m2_tk = Tk()
        mst = [K.sb(es, f"mst{i}", [128, T], BF16) for i in range(2)]
        mst_tk = [Tk(), Tk()]
        for db in range(8):
            b = db % 2
            for i in range(3):
                c0 = CO_GATE + i * 1024 + db * 128
                S.dma("pool", wg[b][i][:], w_in[l, :, c0:c0 + 128].rearrange("(kc p) c -> p kc c", p=128), writes=[w_tk[b]])
                S.dma("pool", wbr[b][i][:], w_branch[l, i, :, db * 128:(db + 1) * 128].rearrange("(kc p) c -> p kc c", p=128), writes=[w_tk[b]])
            for tt in range(4):
                ts = slice(tt * 512, tt * 512 + 512)
                for i in range(3):
                    for kc in range(8):
                        S.op("pe", lambda e, i=i, kc=kc: e.matmul(K.ps[i][:], wg[b][i][:, kc, :], hm[:, kc, ts], start=(kc == 0), stop=(kc == 7)),
                             reads=[w_tk[b], hm_tk], writes=[K.pst[i]])
                    S.op("act", lambda e, i=i: e.activation(sgt[i][:], K.ps[i][:], AF.Sigmoid), reads=[K.pst[i]], writes=[sgt_tk[i]])
                for i in range(3):
                    for kc in range(4):
                        S.op("pe", lambda e, i=i, kc=kc: e.matmul(K.ps[3 + i][:], wbr[b][i][:, kc, :], yT[:, 4 * i + kc, ts], start=(kc == 0), stop=(kc == 3)),
                             reads=[w_tk[b], yT_tk], writes=[K.pst[3 + i]])
                S.op("dve", lambda e: e.tensor_tensor(m1[:], sgt[0][:], K.ps[3][:], ALU.mult), reads=[sgt_tk[0], K.pst[3]], writes=[m1_tk])
                S.op("dve", lambda e: e.tensor_tensor(m2[:], sgt[1][:], K.ps[4][:], ALU.mult), reads=[sgt_tk[1], K.pst[4]], writes=[m2_tk])
                S.op("dve", lambda e: e.tensor_tensor(m1[:], m1[:], m2[:], ALU.add), reads=[m1_tk, m2_tk], writes=[m1_tk])
                S.op("dve", lambda e: e.tensor_tensor(m2[:], sgt[2][:], K.ps[5][:], ALU.mult), reads=[sgt_tk[2], K.pst[5]], writes=[m2_tk])
                S.op("dve", lambda e: e.tensor_tensor(mst[b][:, ts], m1[:], m2[:], ALU.add), reads=[m1_tk, m2_tk], writes=[mst_tk[b]])
            S.dma("sp", mT_d[db * 128:(db + 1) * 128, :], mst[b][:], reads=[mst_tk[b]], writes=[mT_d_tk])
        S.barrier()


def halo_select(K, gathH, gathH_tk, hsel, hsel_tk, haloH, haloH_tk):
    S = K.S
    CW = PW
    with ExitStack() as es:
        sl = [[K.sb(es, f"hs{b}_{i}", [128, CW], BF16) for i in range(3)] for b in range(2)]
        sl_tk = [[Tk() for i in range(3)] for b in range(2)]
        ac = [K.sb(es, f"hacc{b}", [128, CW], BF16) for b in range(2)]
        ac_tk = [Tk(), Tk()]
        for ci in range(NPIECE):
            b = ci % 2
            c0 = ci * CW
            for s_ in range(3):
                S.dma("sp", sl[b][s_][:], gathH[ci, s_ * 128:(s_ + 1) * 128, :], reads=[gathH_tk], writes=[sl_tk[b][s_]])
            S.op("dve", lambda e: e.tensor_scalar(ac[b][:], sl[b][0][:], hsel[:, 0:1], None, ALU.mult),
                 reads=[sl_tk[b][0], hsel_tk], writes=[ac_tk[b]])
            for s_ in (1, 2):
                S.op("dve", lambda e, s_=s_: e.scalar_tensor_tensor(ac[b][:], sl[b][s_][:], hsel[:, s_:s_ + 1], ac[b][:], ALU.mult, ALU.add),
                     reads=[sl_tk[b][s_], hsel_tk, ac_tk[b]], writes=[ac_tk[b]])
            S.dma("sp", haloH[:, c0:c0 + CW], ac[b][:], reads=[ac_tk[b]], writes=[haloH_tk])
        S.barrier()


PARAMS = (("norm_ffn1", [DM]), ("w_ffn1_in", [DM, 2 * DFF]), ("w_ffn1_out", [DFF, DM]), ("norm_mix", [DM]),
          ("w_in", [DM, INW]), ("a_q_norm", [64]), ("a_k_norm", [64]), ("b_q_norm", [64]), ("b_k_norm", [64]),
          ("b_sinks", [8]), ("c_gate_up", [16, 256]), ("c_gate_bias", [256]), ("c_out_norm", [128]),
          ("w_branch", [3, 512, DM]), ("w_out", [DM, DM]), ("norm_ffn2", [DM]), ("w_ffn2_in", [DM, 2 * DFF]),
          ("w_ffn2_out", [DFF, DM]))
INTER = (("qkA", [24, 128, T], BF16), ("vA", [3, 16, 128, 512], BF16), ("qB", [4, 128, T], BF16), ("kB", [128, T], BF16),
         ("vB", [16, 128, 128], BF16), ("qeT", [4, 64, T], BF16), ("keT", [4, 64, T], BF16), ("kd", [64, 32, 256], BF16),
         ("cv", [64, 32, 512], BF16), ("ebl", [64, 128], F32), ("crT", [4, 128, T], BF16))
NLAYER = 2
GROUPS = [[0, 1, 2, 3], [4, 5, 6, 7]]


def build_fused():
    nc = bass.Bass("TRN2", target_bir_lowering=False)
    x_in = nc.dram_tensor("x_in", [DM, T], F32, kind="ExternalInput")
    pos = nc.dram_tensor("pos", [1, T], I32, kind="ExternalInput")
    consts = nc.dram_tensor("consts", [128, NCONST], F32, kind="ExternalInput")
    cf32d = nc.dram_tensor("cf32", [128, 4], F32, kind="ExternalInput")
    m0d = nc.dram_tensor("m0", [128, 256], F32, kind="ExternalInput")
    gmaskd = nc.dram_tensor("gmask", [64, 8], F32, kind="ExternalInput")
    hseld = nc.dram_tensor("hsel", [128, 4], F32, kind="ExternalInput")
    P = {n: nc.dram_tensor(n, [NLAYER] + s, F32, kind="ExternalInput") for n, s in PARAMS}
    x_out = nc.dram_tensor("x_out", [DM, T], F32, kind="ExternalOutput")
    I = {}
    for name, shape, dt in INTER:
        I[name] = nc.dram_tensor(name, shape, dt, kind="Internal")
        I[name + "_tk"] = Tk(name)
    I["expH"] = nc.dram_tensor("expH", [NPIECE, 128, PW], BF16, kind="Internal"); I["expH_tk"] = Tk()
    I["expG"] = nc.dram_tensor("expG", [64, NG], F32, kind="Internal"); I["expG_tk"] = Tk()
    gathH = nc.dram_tensor("gathH", [NPIECE, 4 * 128, PW], BF16, kind="Internal"); gathH_tk = Tk()
    I["allG"] = nc.dram_tensor("gathG", [4 * 64, NG], F32, kind="Internal"); I["allG_tk"] = Tk()
    I["haloH"] = nc.dram_tensor("haloH", [128, NH], BF16, kind="Internal"); I["haloH_tk"] = Tk()
    I["m0"] = m0d
    I["gmask"] = gmaskd
    x1_d = nc.dram_tensor("x1_d", [DM, T], F32, kind="Internal"); x1_tk = Tk()
    xm_d = nc.dram_tensor("xm_d", [DM, T], F32, kind="Internal"); xm_tk = Tk()
    hmT_d = nc.dram_tensor("hmT_d", [DM, T], BF16, kind="Internal"); hmT_tk = Tk()
    mT_d = nc.dram_tensor("mT_d", [DM, T], BF16, kind="Internal"); mT_d_tk = Tk()
    with ExitStack() as es:
        K = KC(nc, es)
        S = K.S
        C = load_consts(K, es, consts)
        cf32 = K.sb(es, "cf32s", [128, 4], F32); cf32_tk = Tk()
        S.dma("sp", cf32[:], cf32d[:, :], writes=[cf32_tk])
        hsel = K.sb(es, "hsel_s", [128, 4], F32); hsel_tk = Tk()
        S.dma("sp", hsel[:], hseld[:, :], writes=[hsel_tk])
        out_tk = Tk()
        for l in range(NLAYER):
            x_src, x_src_tk = (x_in, Tk()) if l == 0 else (xm_d, xm_tk)
            x_dst, x_dst_tk = (xm_d, xm_tk) if l < NLAYER - 1 else (x_out, out_tk)
            with ExitStack() as esl:
                hm = K.sb(esl, "hm", [128, 8, T], BF16); hm_tk = Tk("hm")
                ffn_phase(K, C, l, x_src, x1_d, x_src_tk, x1_tk, P["norm_ffn1"], P["w_ffn1_in"], P["w_ffn1_out"],
                          post=dict(g=P["norm_mix"], hT=hm, hT_tk=hm_tk, h_out=hmT_d, h_out_tk=hmT_tk))
                cosT, cos_tk, sinT, sin_tk = rope_tables(K, esl, pos, cf32, cf32_tk)
                inproj_phase(K, C, l, hm, hm_tk, cosT, cos_tk, sinT, sin_tk, P["w_in"], P, I)
            S.op("pool", lambda e: e.collective_compute("AllGather", ALU.bypass, replica_groups=GROUPS,
                                                       ins=[I["expG"][:, :]], outs=[I["allG"][:, :]]),
                 reads=[I["expG_tk"]], writes=[I["allG_tk"]])
            for pk in range(NPIECE):
                S.op("pool", lambda e, pk=pk: e.collective_compute("AllGather", ALU.bypass, replica_groups=GROUPS,
                                                                   ins=[I["expH"][pk]], outs=[gathH[pk]]),
                     reads=[I["expH_tk"]], writes=[gathH_tk])
            with ExitStack() as es2:
                yT = K.sb(es2, "yT", [128, 12, T], BF16); yT_tk = Tk()
                gla_phase(K, C, l, I, P, yT, yT_tk)
                halo_select(K, gathH, gathH_tk, hsel, hsel_tk, I["haloH"], I["haloH_tk"])
                attn_phase(K, C, l, I, P, yT, yT_tk)
                with ExitStack() as es3:
                    hm2 = K.sb(es3, "hm2", [128, 8, T], BF16); hm2_tk = Tk()
                    S.dma("sp", hm2[:], hmT_d[:, :].rearrange("(kc p) t -> p kc t", p=128), reads=[hmT_tk], writes=[hm2_tk])
                    merge_phase(K, C, l, hm2, hm2_tk, yT, yT_tk, P["w_in"], P["w_branch"], mT_d, mT_d_tk)
            ffn_phase(K, C, l, x1_d, x_dst, x1_tk, x_dst_tk, P["norm_ffn2"], P["w_ffn2_in"], P["w_ffn2_out"],
                      pre=dict(w_out=P["w_out"], mT_d=mT_d, mT_d_tk=mT_d_tk))
        S.finish([out_tk], "sp")
        print("fused program: ninst", S.ninst, "nwaits", S.nwaits, "sem gens", S.gen)
    return nc


def host_cf32():
    inv = (10000.0 ** (-np.arange(0, 64, 2, dtype=np.float32) / 64)).astype(np.float32)
    c = np.zeros((128, 4), np.float32)
    for p in range(128):
        c[p, 0] = inv[p % 32]
        c[p, 1] = -1.0 if (p % 64) < 32 else 1.0
    return c


def host_m0(first):
    c = host_consts()
    m = np.zeros((128, 256), np.float32)
    m[:, 0:128] = 0.0 if first else c[:, C_MASKA:C_MASKA + 128]
    m[:, 128:256] = 0.0 if first else c[:, C_MASKB:C_MASKB + 128]
    return m


def host_gmask(j):
    g = np.zeros((64, 8), np.float32)
    for r in range(4):
        m = 1.0 if r < j else 0.0
        g[:, r] = m
        g[:, 4 + r] = 1.0 - m
    return g


def host_hsel(j):
    h = np.zeros((128, 4), np.float32)
    if j > 0:
        h[:, j - 1] = 1.0
    return h


_PROG = {}


def kernel(**inputs):
    NC = 8
    x = np.asarray(inputs["x"], np.float32)
    positions = np.asarray(inputs["positions"], np.int32)
    consts = host_consts()
    cf32 = host_cf32()
    if "fused" not in _PROG:
        _PROG["fused"] = build_fused()
    params = {n: np.ascontiguousarray(np.asarray(inputs[n], np.float32)) for n, _ in PARAMS}
    maps = []
    for c in range(NC):
        b, j = c // 4, c % 4
        m = dict(x_in=np.ascontiguousarray(x[b, j * T:(j + 1) * T].T),
                 pos=np.ascontiguousarray(positions[b:b + 1, j * T:(j + 1) * T]),
                 consts=consts, cf32=cf32, m0=host_m0(j == 0), gmask=host_gmask(j), hsel=host_hsel(j))
        m.update(params)
        maps.append(m)
    res = run_bass_kernel_spmd(_PROG["fused"], maps, core_ids=list(range(NC))).results
    out = np.zeros((2, 4 * T, DM), np.float32)
    for c in range(NC):
        out[c // 4, (c % 4) * T:(c % 4 + 1) * T, :] = np.asarray(res[c]["x_out"]).T
    return out
```
